# Optimizing a Trainium2 kernel written in Bass

```python
import math
import jax
import jax.numpy as jnp
from jax import lax
import numpy as np

D_MODEL = 1024
BATCH = 8
SEQ = 2048
DEPTH = 1
DEC_BATCH = 32
DEC_SEQ = 32
PAST_LEN = 2048

CHUNK = 64
QBLK = 128
N_A = 8
HD_A = 64
N_B = 4
HD_B = 64
D_A = N_A * HD_A
D_B = N_B * 2 * HD_B
D_MIX = D_A + D_B
D_IN = 3 * D_A + N_A + 3 * D_B
D_FF = 2816
NORM_EPS = 1e-6
NEG_INF = -1e30
LAMBDA_STD = 0.1
ALIBI_SLOPES = (0.25, 0.0625, 0.015625, 0.00390625)

kernel_name = 'streaming_fox_diff_hybrid_step'


def rmsnorm(x, g):
    xf = x.astype(jnp.float32)
    y = xf * lax.rsqrt(jnp.mean(xf * xf, axis=-1, keepdims=True) + NORM_EPS)
    return (y * g.astype(jnp.float32)).astype(x.dtype)


def swiglu(x, w_in, w_out):
    g, u = jnp.split(x @ w_in, 2, axis=-1)
    return (jax.nn.silu(g) * u) @ w_out


def sweep_queries(block_fn, n_q):
    if n_q <= QBLK:
        return block_fn(0, n_q)
    n_blk = n_q // QBLK
    out = lax.map(lambda i: block_fn(i * QBLK, QBLK), jnp.arange(n_blk))
    out = jnp.moveaxis(out, 0, 1)
    return out.reshape(out.shape[0], n_q, *out.shape[3:])


def fox_attention(q, k, v, c_q, c_k, q_off):
    k_pos = jnp.arange(k.shape[1])
    c_k_t = jnp.swapaxes(c_k, 1, 2)
    scale = HD_A ** -0.5

    def block(start, size):
        qb = lax.dynamic_slice_in_dim(q, start, size, axis=1)
        cqb = jnp.swapaxes(lax.dynamic_slice_in_dim(c_q, start, size, axis=1), 1, 2)
        q_pos = q_off + start + jnp.arange(size)
        s = jnp.einsum('bqhd,bkhd->bhqk', qb, k).astype(jnp.float32) * scale
        s = s + (cqb[..., :, None] - c_k_t[..., None, :])
        s = jnp.where(q_pos[:, None] >= k_pos[None, :], s, NEG_INF)
        p = jax.nn.softmax(s, axis=-1).astype(v.dtype)
        return jnp.einsum('bhqk,bkhd->bqhd', p, v)

    return sweep_queries(block, q.shape[1])


def diff_attention(q, k, v, lam, q_off):
    k_pos = jnp.arange(k.shape[1])
    slopes = jnp.asarray(ALIBI_SLOPES, jnp.float32)
    scale = HD_B ** -0.5

    def block(start, size):
        qb = lax.dynamic_slice_in_dim(q, start, size, axis=1)
        q_pos = q_off + start + jnp.arange(size)
        dist = jnp.abs(q_pos[:, None] - k_pos[None, :]).astype(jnp.float32)
        bias = -slopes[:, None, None] * dist
        s = jnp.einsum('bqhed,bkhed->behqk', qb, k).astype(jnp.float32) * scale + bias
        mask = (q_pos // CHUNK)[:, None] >= (k_pos // CHUNK)[None, :]
        s = jnp.where(mask, s, NEG_INF)
        p = jax.nn.softmax(s, axis=-1)
        a = (p[:, 0] - lam * p[:, 1]).astype(v.dtype)
        return jnp.einsum('bhqk,bkhv->bqhv', a, v)

    return sweep_queries(block, q.shape[1])


def hybrid_layer(x, past_fk, past_fv, past_flf, past_dk, past_dv, lambda_init,
                 norm_ffn1, w_ffn1_in, w_ffn1_out, norm_mix, w_in, b_forget,
                 lambda_q1, lambda_k1, lambda_q2, lambda_k2, diff_subln, w_out,
                 norm_ffn2, w_ffn2_in, w_ffn2_out):
    B, T, _ = x.shape
    past = past_fk.shape[1]
    x = x + 0.5 * swiglu(rmsnorm(x, norm_ffn1), w_ffn1_in, w_ffn1_out)
    h = rmsnorm(x, norm_mix)
    proj = h @ w_in
    splits = [D_A, 2 * D_A, 3 * D_A, 3 * D_A + N_A, 3 * D_A + N_A + D_B, 3 * D_A + N_A + 2 * D_B]
    qa, ka, va, fa, qb, kb, vb = jnp.split(proj, splits, axis=-1)
    qa = qa.reshape(B, T, N_A, HD_A)
    ka = ka.reshape(B, T, N_A, HD_A)
    va = va.reshape(B, T, N_A, HD_A)
    logf = jax.nn.log_sigmoid((fa + b_forget).astype(jnp.float32))
    qb = qb.reshape(B, T, N_B, 2, HD_B)
    kb = kb.reshape(B, T, N_B, 2, HD_B)
    vb = vb.reshape(B, T, N_B, 2 * HD_B)
    k_all = jnp.concatenate([past_fk, ka], axis=1)
    v_all = jnp.concatenate([past_fv, va], axis=1)
    c_all = jnp.cumsum(jnp.concatenate([past_flf.astype(jnp.float32), logf], axis=1), axis=1)
    o_a = fox_attention(qa, k_all, v_all, c_all[:, past:], c_all, past)
    lam = (jnp.exp(jnp.sum(lambda_q1.astype(jnp.float32) * lambda_k1.astype(jnp.float32)))
           - jnp.exp(jnp.sum(lambda_q2.astype(jnp.float32) * lambda_k2.astype(jnp.float32)))
           + lambda_init)
    kb_all = jnp.concatenate([past_dk, kb], axis=1)
    vb_all = jnp.concatenate([past_dv, vb], axis=1)
    o_b = diff_attention(qb, kb_all, vb_all, lam, past)
    o_b = rmsnorm(o_b, diff_subln) * (1.0 - lambda_init)
    o = jnp.concatenate([o_a.reshape(B, T, D_A), o_b.reshape(B, T, D_B)], axis=-1)
    x = x + o @ w_out
    x = x + 0.5 * swiglu(rmsnorm(x, norm_ffn2), w_ffn2_in, w_ffn2_out)
    return x, ka, va, logf, kb, vb


def setup_inputs(seed: int = 0) -> dict:
    key = jax.random.key(seed)
    ks = jax.random.split(key, 24)
    f32 = jnp.float32

    def nrm(k, shape, scale=1.0):
        return jax.random.normal(k, shape, f32) * scale

    def gain(k, n):
        return 1.0 + 0.02 * jax.random.normal(k, (DEPTH, n), f32)

    return {
        'x_prompt': nrm(ks[0], (BATCH, SEQ, D_MODEL)),
        'x_sample': nrm(ks[1], (DEC_BATCH, DEC_SEQ, D_MODEL)),
        'cache_fox_k': nrm(ks[2], (DEPTH, DEC_BATCH, PAST_LEN, N_A, HD_A)),
        'cache_fox_v': nrm(ks[3], (DEPTH, DEC_BATCH, PAST_LEN, N_A, HD_A)),
        'cache_fox_logf': jax.nn.log_sigmoid(2.5 + nrm(ks[4], (DEPTH, DEC_BATCH, PAST_LEN, N_A))),
        'cache_diff_k': nrm(ks[5], (DEPTH, DEC_BATCH, PAST_LEN, N_B, 2, HD_B)),
        'cache_diff_v': nrm(ks[6], (DEPTH, DEC_BATCH, PAST_LEN, N_B, 2 * HD_B)),
        'norm_ffn1': gain(ks[7], D_MODEL),
        'w_ffn1_in': nrm(ks[8], (DEPTH, D_MODEL, 2 * D_FF), D_MODEL ** -0.5),
        'w_ffn1_out': nrm(ks[9], (DEPTH, D_FF, D_MODEL), D_FF ** -0.5),
        'norm_mix': gain(ks[10], D_MODEL),
        'w_in': nrm(ks[11], (DEPTH, D_MODEL, D_IN), D_MODEL ** -0.5),
        'b_forget': jax.random.uniform(ks[12], (DEPTH, N_A), f32, 1.0, 4.0),
        'lambda_q1': nrm(ks[13], (DEPTH, HD_B), LAMBDA_STD),
        'lambda_k1': nrm(ks[14], (DEPTH, HD_B), LAMBDA_STD),
        'lambda_q2': nrm(ks[15], (DEPTH, HD_B), LAMBDA_STD),
        'lambda_k2': nrm(ks[16], (DEPTH, HD_B), LAMBDA_STD),
        'diff_subln': gain(ks[17], 2 * HD_B),
        'w_out': nrm(ks[18], (DEPTH, D_MIX, D_MODEL), D_MIX ** -0.5),
        'norm_ffn2': gain(ks[19], D_MODEL),
        'w_ffn2_in': nrm(ks[20], (DEPTH, D_MODEL, 2 * D_FF), D_MODEL ** -0.5),
        'w_ffn2_out': nrm(ks[21], (DEPTH, D_FF, D_MODEL), D_FF ** -0.5),
        'norm_final': 1.0 + 0.02 * jax.random.normal(ks[22], (D_MODEL,), f32),
    }


def reference(x_prompt, x_sample, cache_fox_k, cache_fox_v, cache_fox_logf, cache_diff_k, cache_diff_v,
              norm_ffn1, w_ffn1_in, w_ffn1_out, norm_mix, w_in, b_forget,
              lambda_q1, lambda_k1, lambda_q2, lambda_k2, diff_subln, w_out,
              norm_ffn2, w_ffn2_in, w_ffn2_out, norm_final):
    B = x_prompt.shape[0]
    dt = x_prompt.dtype
    xp, xs = x_prompt, x_sample
    pk, pv, plf, pdk, pdv = [], [], [], [], []
    sk, sv, slf, sdk, sdv = [], [], [], [], []
    for l in range(DEPTH):
        lambda_init = 0.8 - 0.6 * math.exp(-0.3 * l)
        w = (norm_ffn1[l], w_ffn1_in[l], w_ffn1_out[l], norm_mix[l], w_in[l], b_forget[l],
             lambda_q1[l], lambda_k1[l], lambda_q2[l], lambda_k2[l], diff_subln[l], w_out[l],
             norm_ffn2[l], w_ffn2_in[l], w_ffn2_out[l])
        xp, ka, va, lfa, kb, vb = hybrid_layer(
            xp,
            jnp.zeros((B, 0, N_A, HD_A), dt), jnp.zeros((B, 0, N_A, HD_A), dt),
            jnp.zeros((B, 0, N_A), jnp.float32),
            jnp.zeros((B, 0, N_B, 2, HD_B), dt), jnp.zeros((B, 0, N_B, 2 * HD_B), dt),
            lambda_init, *w)
        pk.append(ka); pv.append(va); plf.append(lfa); pdk.append(kb); pdv.append(vb)
        xs, ka, va, lfa, kb, vb = hybrid_layer(
            xs, cache_fox_k[l], cache_fox_v[l], cache_fox_logf[l], cache_diff_k[l], cache_diff_v[l],
            lambda_init, *w)
        sk.append(ka); sv.append(va); slf.append(lfa); sdk.append(kb); sdv.append(vb)
    y_prompt = rmsnorm(xp, norm_final)
    y_sample = rmsnorm(xs, norm_final)
    return (y_prompt, y_sample,
            jnp.stack(pk), jnp.stack(pv), jnp.stack(plf), jnp.stack(pdk), jnp.stack(pdv),
            jnp.stack(sk), jnp.stack(sv), jnp.stack(slf), jnp.stack(sdk), jnp.stack(sdv))
```

```python
import contextlib
import numpy as np
import concourse.bass as bass
import concourse.mybir as mybir
from concourse.bass_utils import run_bass_kernel_spmd

F32 = mybir.dt.float32
BF16 = mybir.dt.bfloat16
AF = mybir.ActivationFunctionType
ALU = mybir.AluOpType
AX = mybir.AxisListType

NT = 17
T = 2176
TP = 2048
DM = 1024
DFF = 2816
NCH = 22
EPS = 1e-6
SUBS = [(0, 512), (512, 512), (1024, 512), (1536, 512), (2048, 128)]
QUARTERS = [(0, 6), (6, 6), (12, 5), (17, 5)]
SLOPES = (0.25, 0.0625, 0.015625, 0.00390625)
LAMBDA_INIT = 0.2
NEG = -30000.0
NDUMMY = 0
DBG = {}


class Op:
    __slots__ = ("eng", "fn", "deps", "is_dma", "sig", "idx", "dsem", "dval", "pdeps", "need")


class Prog:
    ENGS = ("pe", "act", "dve", "pool", "sp")
    SAME_ENGINE_SYNC = ("act", "dve", "pool")

    def __init__(self, nc, n_dma_sems=12):
        self.nc = nc
        self.ops = {e: [] for e in self.ENGS}
        self.last_w = {}
        self.readers = {}
        self.fences = {}
        self.n_dma_sems = n_dma_sems
        self.dma_count = {e: 0 for e in self.ENGS}
        self.all_dma = []

    def add(self, eng, fn, reads=(), writes=(), dma=False):
        op = Op()
        op.eng, op.fn, op.is_dma = eng, fn, dma
        op.sig = None
        op.need = False
        op.dsem = op.dval = None
        deps = set()
        for k in reads:
            w = self.last_w.get(k)
            if w is None:
                w = self.fences.get(k[0])
            if w is not None:
                deps.add(w)
        for k in writes:
            w = self.last_w.get(k)
            if w is None:
                w = self.fences.get(k[0])
            if w is not None:
                deps.add(w)
            for r in self.readers.get(k, ()):
                deps.add(r)
        op.deps = deps
        for k in writes:
            self.last_w[k] = op
            self.readers[k] = []
        for k in reads:
            self.readers.setdefault(k, []).append(op)
        op.idx = len(self.ops[eng])
        self.ops[eng].append(op)
        if dma:
            j = self.dma_count[eng]
            self.dma_count[eng] = j + 1
            op.dsem = (eng, j % self.n_dma_sems)
            op.dval = 16 * (j // self.n_dma_sems + 1)
            self.all_dma.append(op)
        return op

    def op(self, eng, method, *args, reads=(), writes=(), **kw):
        return self.add(eng, lambda e: getattr(e, method)(*args, **kw), reads, writes)

    def dma(self, q, out, in_, reads=(), writes=()):
        return self.add(q, lambda e: e.dma_start(out=out, in_=in_), reads, writes, dma=True)

    def realloc(self, old_names, new_names, scratch):
        old = set(old_names)
        keys = [k for k in set(self.last_w) | set(self.readers) if k[0] in old]
        f = self.add("pool", lambda e: e.memset(scratch, 0.0), reads=(), writes=keys + [("fence_scratch",)])
        for n in new_names:
            self.fences[n] = f
        for k in keys:
            self.last_w.pop(k, None)
            self.readers.pop(k, None)
        return f

    def finalize_and_emit(self):
        nc = self.nc
        for eng in self.ENGS:
            for op in self.ops[eng]:
                best = {}
                pd = []
                for d in op.deps:
                    if d.is_dma:
                        pd.append(d)
                    else:
                        if d.eng == eng and eng not in self.SAME_ENGINE_SYNC:
                            continue
                        b = best.get(d.eng)
                        if b is None or d.idx > b.idx:
                            best[d.eng] = d
                pd.extend(best.values())
                op.pdeps = pd
                for d in pd:
                    d.need = True
        for eng in self.ENGS:
            c = 0
            for op in self.ops[eng]:
                if op.need and not op.is_dma:
                    c += 1
                    op.sig = c
        with contextlib.ExitStack() as st:
            esem = {e: st.enter_context(nc.semaphore("s_" + e)) for e in ("pe", "act", "dve", "pool")}
            dsem = {}
            for e in self.ENGS:
                for i in range(min(self.n_dma_sems, self.dma_count[e])):
                    dsem[(e, i)] = st.enter_context(nc.semaphore("d_%s_%d" % (e, i)))
            block = st.enter_context(nc.Block())
            final_dma = {}
            for op in self.all_dma:
                final_dma[op.dsem] = max(final_dma.get(op.dsem, 0), op.dval)

            def emit(eng, e):
                waited = {}
                for op in self.ops[eng]:
                    for d in op.pdeps:
                        if d.is_dma:
                            key, val, sem = d.dsem, d.dval, dsem[d.dsem]
                        else:
                            key, val, sem = d.eng, d.sig, esem[d.eng]
                        if waited.get(key, 0) < val:
                            e.wait_ge(sem, val)
                            waited[key] = val
                    if op.is_dma and op.dval > 16 and waited.get(op.dsem, 0) < op.dval - 16:
                        e.wait_ge(dsem[op.dsem], op.dval - 16)
                        waited[op.dsem] = op.dval - 16
                    ins = op.fn(e)
                    if op.is_dma:
                        ins.then_inc(dsem[op.dsem], 16)
                    elif op.sig is not None:
                        ins.then_inc(esem[eng], 1)
                if eng == "sp":
                    for key, val in final_dma.items():
                        if waited.get(key, 0) < val:
                            e.wait_ge(dsem[key], val)

            @block.tensor
            def _(e):
                emit("pe", e)

            @block.scalar
            def _(e):
                emit("act", e)

            @block.vector
            def _(e):
                emit("dve", e)

            @block.gpsimd
            def _(e):
                emit("pool", e)

            @block.sync
            def _(e):
                emit("sp", e)


def build_nc(stage=99):
    nc = bass.Bass("TRN2", target_bir_lowering=False)

    def din(name, shape, dt=F32):
        return nc.dram_tensor(name, list(shape), dt, kind="ExternalInput").ap()

    def dout(name, shape):
        return nc.dram_tensor(name, list(shape), F32, kind="ExternalOutput").ap()

    xin = din("xin", [T, DM])
    w1a = din("w1a", [NCH, 128, 2048])
    w1b = din("w1b", [128, NCH, 1024])
    w2a = din("w2a", [NCH, 128, 2048])
    w2b = din("w2b", [128, NCH, 1024])
    wqk = din("wqk", [8, 128, 2048])
    wkv = din("wkv", [8, 128, 2048])
    wfa = din("wfa", [128, 64])
    wo = din("wo", [8, 128, 1024])
    gcols_d = din("gcols", [128, 24])
    gvec_d = din("gvec", [3, DM])
    gfin_d = din("gfin", [1, DM])
    bfg_d = din("bfg", [1, NT * 8])
    lam_d = din("lamv", [1, 256])
    subln_d = din("subln", [1, 128])
    ident_d = din("ident", [128, 128])
    dmask_d = din("dmask", [128, 5, 128])
    alq_d = din("alq", [4, 6, T])
    alk_d = din("alk", [4, 6, T])
    kTc_d = din("kTc", [8, 4, 128, TP])
    vc_d = din("vc", [8, 4, 128, 16, 128])
    plf_d = din("plf", [32, TP])
    y_d = dout("y", [T, DM])
    kv_d = dout("kv", [8, T, 256])
    lfo_d = dout("lfo", [T, 8])
    caugP = nc.dram_tensor("caugP", [2, 8, 3, TP], BF16, kind="Internal").ap()
    caugS = nc.dram_tensor("caugS", [2, 4, 8, 3, TP + 32], BF16, kind="Internal").ap()

    P = Prog(nc)

    def sb(name, shape, dt):
        return nc.alloc_sbuf_tensor("s_" + name, list(shape), dt)

    x = sb("x", [128, NT, DM], F32)
    hT = sb("hT", [128, 8, T], BF16)
    identb = sb("identb", [128, 128], BF16)
    identf = sb("identf", [128, 128], F32)
    dm = sb("dm", [128, 5, 128], BF16)
    gcols = sb("gcols", [128, 24], F32)
    ss = sb("ss", [128, NT], F32)
    rstd = sb("rstd", [128, NT], F32)
    hn = [sb("hn%d" % i, [128, DM], BF16) for i in range(2)]
    junk = sb("junk", [128, DM], BF16)
    fscr = sb("fscr", [128, 8], F32)
    epsc = sb("epsc", [128, 1], F32)
    PS = [nc.alloc_psum_tensor("ps%d" % i, [128, 512], F32) for i in range(7)]
    TR = nc.alloc_psum_tensor("trb", [128, 1024], BF16)

    xin_v = xin.rearrange("(j p) d -> p j d", p=128)
    for t in range(NT):
        P.dma("sp", x[:, t, :], xin_v[:, t, :], writes=[("x", t)])
    P.dma("pool", identb[:, :], ident_d[:, :], writes=[("identb",)])
    P.dma("sp", identf[:, :], ident_d[:, :], writes=[("identf",)])
    P.dma("pool", dm[:, :, :], dmask_d[:, :, :], writes=[("dm",)])
    P.dma("sp", gcols[:, :], gcols_d[:, :], writes=[("gcols",)])

    P.op("pool", "memset", epsc[:, :], EPS, writes=[("epsc",)])
    TRV = [(TR[:, :], ("TR", 0)), (PS[6][:, :].bitcast(BF16), ("ps", 6)), (PS[5][:, :].bitcast(BF16), ("ps", 5))]
    trc = [0]

    def rstd_all():
        for t in range(NT):
            P.op("act", "activation", junk[:, :], x[:, t, :], AF.Square, accum_out=ss[:, t:t + 1],
                 reads=[("x", t)], writes=[("junk",), ("ss",)])
        P.op("act", "activation", rstd[:, :], ss[:, :], AF.Ln, scale=1.0 / DM, bias=epsc[:, 0:1],
             reads=[("ss",), ("epsc",)], writes=[("rstd",)])
        P.op("act", "activation", rstd[:, :], rstd[:, :], AF.Exp, scale=-0.5,
             reads=[("rstd",)], writes=[("rstd",)])

    def norm_to_hT(gi, gt):
        P.dma("sp", gt[:, :], gvec_d[gi:gi + 1, :].broadcast_to([128, DM]), writes=[("gt",)])
        rstd_all()
        for t in range(NT):
            h = hn[t % 2]
            P.op("dve", "scalar_tensor_tensor", h[:, :], x[:, t, :], rstd[:, t:t + 1], gt[:, :], ALU.mult, ALU.mult,
                 reads=[("x", t), ("rstd",), ("gt",)], writes=[("hn", t % 2)])
            for half in range(2):
                trv, trk = TRV[trc[0] % 3]
                trc[0] += 1
                for kk in range(4):
                    k = half * 4 + kk
                    P.op("pe", "transpose", trv[:, kk * 128:(kk + 1) * 128], h[:, k * 128:(k + 1) * 128],
                         identb[:, :], reads=[("hn", t % 2), ("identb",)], writes=[trk])
                P.op("act", "activation", hT[:, half * 4:half * 4 + 4, t * 128:(t + 1) * 128],
                     trv[:, 0:512].rearrange("p (k c) -> p k c", k=4), AF.Copy,
                     reads=[trk], writes=[("hT", t, half * 4 + kk) for kk in range(4)])

    def hTk(t):
        return [("hT", t, k) for k in range(8)]

    def tiles_of(s0, sn):
        return list(range(s0 // 128, (s0 + sn) // 128))

    def ffn(wa, wb, gi, tagn):
        st = contextlib.ExitStack()
        actT = st.enter_context(nc.sbuf_tensor("s_actT" + tagn, [128, 6, T], BF16))
        wbq = [st.enter_context(nc.sbuf_tensor("s_wbq%s%d" % (tagn, i), [128, 6, 1024], BF16)) for i in range(2)]
        wab = [st.enter_context(nc.sbuf_tensor("s_wab%s%d" % (tagn, i), [128, 8, 256], BF16)) for i in range(3)]
        sg = [st.enter_context(nc.sbuf_tensor("s_sg%s%d" % (tagn, i), [128, 512], F32)) for i in range(2)]
        gt = st.enter_context(nc.sbuf_tensor("s_gt" + tagn, [128, DM], F32))
        names = ["actT", "wbq", "wab", "sg", "gt"]
        if not DBG.get("no_norm"):
            norm_to_hT(gi, gt)
        cnt = 0
        pcnt = 0
        for qi, (c0, nq) in enumerate(QUARTERS[:DBG.get("nq", 4)]):
            wq_ = wbq[qi % 2]
            P.dma("pool", wq_[:, 0:nq, :], wb[:, c0:c0 + nq, :], writes=[("wbq", qi % 2)])
            for cl in range(nq):
                c = c0 + cl
                wbuf = wab[c % 3]
                P.dma("pool", wbuf[:, :, :], wa[c].rearrange("p (k j) -> p k j", k=8), writes=[("wab", c % 3)])
                for (s0, sn) in SUBS:
                    pg = PS[cnt % 2]
                    pu = PS[2 + cnt % 2]
                    hk = [kk_ for t in tiles_of(s0, sn) for kk_ in hTk(t)]
                    for k in range(8):
                        P.op("pe", "matmul", pg[:, 0:sn], wbuf[:, k, 0:128], hT[:, k, s0:s0 + sn],
                             start=(k == 0), stop=(k == 7),
                             reads=[("wab", c % 3)] + hk, writes=[("ps", cnt % 2)])
                    for k in range(8):
                        P.op("pe", "matmul", pu[:, 0:sn], wbuf[:, k, 128:256], hT[:, k, s0:s0 + sn],
                             start=(k == 0), stop=(k == 7),
                             reads=[("wab", c % 3)] + hk, writes=[("ps", 2 + cnt % 2)])
                    s_ = sg[cnt % 2]
                    P.op("act", "activation", s_[:, 0:sn], pg[:, 0:sn], AF.Silu,
                         reads=[("ps", cnt % 2)], writes=[("sg", cnt % 2)])
                    P.op("dve", "tensor_tensor", actT[:, cl, s0:s0 + sn], s_[:, 0:sn], pu[:, 0:sn], ALU.mult,
                         reads=[("sg", cnt % 2), ("ps", 2 + cnt % 2)],
                         writes=[("actT", cl, t) for t in tiles_of(s0, sn)])
                    cnt += 1
            for t in range(NT if DBG.get("phaseB", 1) else 0):
                for dh in range(2):
                    po = PS[4 + pcnt % 2]
                    for cl in range(nq):
                        P.op("pe", "matmul", po[:, :], actT[:, cl, t * 128:(t + 1) * 128],
                             wq_[:, cl, dh * 512:(dh + 1) * 512], start=(cl == 0), stop=(cl == nq - 1),
                             reads=[("actT", cl, t), ("wbq", qi % 2)], writes=[("ps", 4 + pcnt % 2)])
                    P.op("dve", "scalar_tensor_tensor", x[:, t, dh * 512:(dh + 1) * 512], po[:, :], 0.5,
                         x[:, t, dh * 512:(dh + 1) * 512], ALU.mult, ALU.add,
                         reads=[("ps", 4 + pcnt % 2), ("x", t)], writes=[("x", t)])
                    pcnt += 1
        st.close()
        return names

    old_names = ["actT", "wbq", "wab", "sg", "gt"]
    if stage >= 1:
        ffn(w1a, w1b, 0, "a")

    C_NAMES = ["gt", "wfab", "lf", "lfw", "bft", "lfT", "cs", "cc", "r1", "ones_b", "c3", "c3n"]
    U_NAMES = ["qT", "kT", "Vt", "Vn", "kTc", "Vc", "wqkb", "wkvb", "wob", "pT", "kvst", "ou", "oS", "oT",
               "den", "t1", "od", "sq", "rcp", "g8", "lamt", "lamw", "lam"]
    stk = [None]

    def sbp(name, shape, dt):
        return stk[0].enter_context(nc.sbuf_tensor("s_" + name, list(shape), dt))

    def phase_c():
        st = contextlib.ExitStack()
        stk[0] = st

        wfab = sbp("wfab", [128, 8, 8], BF16)
        lf = sbp("lf", [128, NT * 8], F32)
        lfw = [sbp("lfw%d" % i, [128, NT * 8], F32) for i in range(3)]
        bft = sbp("bft", [128, NT * 8], F32)
        lfT = sbp("lfT", [8, T], F32)
        cs = sbp("cs", [40, TP + 32], F32)
        cc = sbp("cc", [40, TP + 32], F32)
        r1 = sbp("r1", [40, TP + 32], F32)
        ones_b = sbp("ones_b", [40, TP + 32], BF16)
        c3 = sbp("c3", [40, 3, TP + 32], BF16)
        c3n = sbp("c3n", [40, 3, TP + 32], BF16)
        c_names = ["wfab", "lf", "lfw", "bft", "lfT", "cs", "cc", "r1", "ones_b", "c3", "c3n"]
        P.realloc(old_names, C_NAMES, fscr[0:1, 0:1])
        P.dma("sp", cs[0:32, 0:TP], plf_d[:, :], writes=[("cs",)])

        gtc = sbp("gtc", [128, DM], F32)
        norm_to_hT(1, gtc)

        P.dma("sp", bft[:, :], bfg_d[0:1, :].broadcast_to([128, NT * 8]), writes=[("bft",)])
        P.dma("pool", wfab[:, :, :], wfa.rearrange("p (k j) -> p k j", k=8), writes=[("wfab",)])
        pf = PS[3]
        for t in range(NT):
            for k in range(8):
                P.op("pe", "matmul", pf[:, t * 8:(t + 1) * 8], hT[:, k, t * 128:(t + 1) * 128], wfab[:, k, :],
                     start=(k == 0), stop=(k == 7), skip_group_check=True,
                     reads=hTk(t) + [("wfab",)], writes=[("ps", 3)])
        NF = NT * 8
        P.op("dve", "tensor_tensor", lfw[0][:, :], pf[:, 0:NF], bft[:, :], ALU.add,
             reads=[("ps", 3), ("bft",)], writes=[("lfw", 0)])
        P.op("act", "activation", lfw[1][:, :], lfw[0][:, :], AF.Abs,
             reads=[("lfw", 0)], writes=[("lfw", 1)])
        P.op("act", "activation", lfw[1][:, :], lfw[1][:, :], AF.Exp, scale=-1.0,
             reads=[("lfw", 1)], writes=[("lfw", 1)])
        P.op("act", "activation", lfw[1][:, :], lfw[1][:, :], AF.Ln, bias=1.0,
             reads=[("lfw", 1)], writes=[("lfw", 1)])
        P.op("dve", "tensor_scalar", lfw[2][:, :], lfw[0][:, :], 0.0, None, ALU.min,
             reads=[("lfw", 0)], writes=[("lfw", 2)])
        P.op("dve", "tensor_tensor", lf[:, :], lfw[2][:, :], lfw[1][:, :], ALU.subtract,
             reads=[("lfw", 1), ("lfw", 2)], writes=[("lf",)])
        P.dma("sp", lfo_d.rearrange("(j p) h -> p j h", p=128), lf[:, :].rearrange("p (j h) -> p j h", h=8),
              reads=[("lf",)])
        P.op("pool", "memset", cs[32:40, TP:TP + 32], 0.0, writes=[("cs", "pz")])
        for g in range(5):
            n = 4 if g < 4 else 1
            for jj in range(n):
                t = g * 4 + jj
                P.op("pe", "transpose", PS[3][0:8, jj * 128:(jj + 1) * 128], lf[:, t * 8:(t + 1) * 8], identf[:, :],
                     reads=[("lf",), ("identf",)], writes=[("ps", 3)])
            if g < 4:
                P.op("dve", "tensor_copy", cs[32:40, g * 512:(g + 1) * 512], PS[3][0:8, 0:512],
                     reads=[("ps", 3)], writes=[("cs", "p", g)])
            else:
                P.op("dve", "tensor_copy", lfT[:, TP:T], PS[3][0:8, 0:128],
                     reads=[("ps", 3)], writes=[("lfT",)])
        P.op("pool", "memset", ones_b[:, :], 1.0, writes=[("ones_b",)])
        for b in range(4):
            P.dma("sp", cs[8 * b:8 * b + 8, TP:TP + 32], lfT[0:8, TP + 32 * b: TP + 32 * b + 32],
                  reads=[("lfT",)], writes=[("cs", "s", b)])
        P.op("dve", "tensor_tensor_scan", cc[:, :], ones_b[:, :], cs[:, :], 0.0, ALU.mult, ALU.add,
             reads=[("cs",), ("cs", "pz"), ("ones_b",)] + [("cs", "p", g) for g in range(4)]
             + [("cs", "s", b) for b in range(4)], writes=[("cc",)])
        n = TP + 32
        P.op("dve", "tensor_copy", c3[:, 0, :], cc[:, :], reads=[("cc",)], writes=[("c3",)])
        P.op("dve", "tensor_tensor", r1[:, :], cc[:, :], c3[:, 0, :], ALU.subtract,
             reads=[("cc",), ("c3",)], writes=[("r1",)])
        P.op("dve", "tensor_copy", c3[:, 1, :], r1[:, :], reads=[("r1",)], writes=[("c3",)])
        P.op("dve", "tensor_tensor", r1[:, :], r1[:, :], c3[:, 1, :], ALU.subtract,
             reads=[("r1",), ("c3",)], writes=[("r1",)])
        P.op("dve", "tensor_copy", c3[:, 2, :], r1[:, :], reads=[("r1",)], writes=[("c3",)])
        P.op("dve", "tensor_scalar", c3n[:, :, :], c3[:, :, :], -1.0, None, ALU.mult,
             reads=[("c3",)], writes=[("c3n",)])
        P.dma("sp", caugP[0], c3[32:40, :, 0:TP], reads=[("c3",)], writes=[("caug", 0)])
        P.dma("sp", caugP[1], c3n[32:40, :, 0:TP], reads=[("c3n",)], writes=[("caug", 1)])
        P.dma("sp", caugS[0].rearrange("b h r s -> (b h) r s"), c3[0:32, :, :], reads=[("c3",)], writes=[("caug", 2)])
        P.dma("sp", caugS[1].rearrange("b h r s -> (b h) r s"), c3n[0:32, :, :], reads=[("c3n",)], writes=[("caug", 3)])

    def phase_u():
        stk[0].close()
        st = contextlib.ExitStack()
        stk[0] = st
        qT = [sbp("qT%d" % i, [70, 2, T], BF16) for i in range(1)]
        kT = [sbp("kT%d" % i, [70, 2, T], BF16) for i in range(1)]
        Vt = [sbp("Vt%d" % i, [128, 16, 132], BF16) for i in range(1)]
        Vn = [sbp("Vn%d" % i, [32, 4, 132], BF16) for i in range(1)]
        kTc = [sbp("kTc%d" % i, [70, 2, TP], BF16) for i in range(2)]
        Vc = [sbp("Vc%d" % i, [128, 16, 132], BF16) for i in range(2)]
        wqkb = [sbp("wqkb%d" % i, [128, 8, 256], BF16) for i in range(2)]
        wkvb = [sbp("wkvb%d" % i, [128, 8, 256], BF16) for i in range(2)]
        wob = [sbp("wob%d" % i, [128, 1024], BF16) for i in range(4)]
        pT = [sbp("pT%d" % i, [128, 512], BF16) for i in range(4)]
        kvst = [sbp("kvst%d" % i, [128, 256], F32) for i in range(3)]
        ou = sbp("ou", [128, 16, 128], BF16)
        oS = sbp("oS", [32, 4, 128], BF16)
        oT = sbp("oT", [128, 2, T], BF16)
        den = sbp("den", [128, 64], F32)
        t1 = sbp("t1", [128, 128], F32)
        od = [sbp("od_%d" % i, [128, 128], F32) for i in range(4)]
        ods = [sbp("ods", [32, 128], F32)]
        sq = sbp("sqj", [128, 128], F32)
        rcp = sbp("rcp", [128, 2], F32)
        g8 = sbp("g8", [128, 128], F32)
        lamt = oT[:, 0, 0:512].bitcast(F32)
        lamw = sbp("lamw", [128, 8], F32)
        new_names = ["qT", "kT", "Vt", "Vn", "kTc", "Vc", "wqkb", "wkvb", "wob", "pT", "kvst", "ou", "oS", "oT",
                     "den", "t1", "od", "sq", "rcp", "g8", "lamt", "lamw", "lam"]
        P.realloc(C_NAMES, U_NAMES, fscr[0:1, 2:3])

        P.dma("sp", lamt[:, :], lam_d[0:1, :].broadcast_to([128, 256]), writes=[("oT", 0, 0)])
        P.dma("sp", g8[:, :], subln_d[0:1, :].broadcast_to([128, 128]), writes=[("g8",)])
        P.op("dve", "tensor_scalar", g8[:, :], g8[:, :], 1.0 - LAMBDA_INIT, None, ALU.mult,
             reads=[("g8",)], writes=[("g8",)])
        P.op("dve", "tensor_tensor", lamt[:, 0:64], lamt[:, 0:64], lamt[:, 64:128], ALU.mult,
             reads=[("oT", 0, 0)], writes=[("oT", 0, 0)])
        P.op("dve", "tensor_tensor", lamt[:, 128:192], lamt[:, 128:192], lamt[:, 192:256], ALU.mult,
             reads=[("oT", 0, 0)], writes=[("oT", 0, 0)])
        P.op("dve", "reduce_sum", lamw[:, 0:1], lamt[:, 0:64], AX.X, reads=[("oT", 0, 0)], writes=[("lamw",)])
        P.op("dve", "reduce_sum", lamw[:, 1:2], lamt[:, 128:192], AX.X, reads=[("oT", 0, 0)], writes=[("lamw",)])
        P.op("act", "activation", lamw[:, 2:4], lamw[:, 0:2], AF.Exp, reads=[("lamw",)], writes=[("lamw",)])
        P.op("dve", "tensor_tensor", lamw[:, 4:5], lamw[:, 2:3], lamw[:, 3:4], ALU.subtract,
             reads=[("lamw",)], writes=[("lamw",)])
        P.op("dve", "tensor_scalar", lamw[:, 5:6], lamw[:, 4:5], LAMBDA_INIT, None, ALU.add,
             reads=[("lamw",)], writes=[("lam",)])
        lam_ap = lamw[:, 5:6]


        kvcnt = [0]
        TRU = [(TR[:, :], ("TR", 0)), (PS[0][:, :].bitcast(BF16), ("ps", 0))]
        TRF = TR[:, :].bitcast(F32)
        ptc = [0]
        sbank = [0]
        pocnt = [0]

        dcnt = [0]

        def finalize(u, npart, items, rk, odt):
            fox = u < 4
            di_ = dcnt[0] % 8
            dcnt[0] += 1
            d0 = 8 * di_
            n = len(items)
            dk = ("den", di_)

            def s1():
                for si_, (Oreg, out_ap, slot) in enumerate(items):
                    if fox:
                        for m in range(2):
                            O = Oreg(m)
                            dkm = dk + (si_, m)
                            dn = den[0:npart, d0 + 2 * si_ + m:d0 + 2 * si_ + m + 1]
                            P.op("dve", "reciprocal", dn, O[:, 64:65], reads=rk, writes=[dkm])
                            P.op("dve", "tensor_scalar", out_ap[:, 64 * m:64 * m + 64], O[:, 0:64], dn, None,
                                 ALU.mult, reads=rk + [dkm], writes=[slot + (m,)])
                    else:
                        O0, O1 = Oreg(0), Oreg(1)
                        r0 = rcp[0:npart, 0:1]
                        r1_ = rcp[0:npart, 1:2]
                        P.op("dve", "reciprocal", r0, O0[:, 128:129], reads=rk, writes=[("rcp", 0)])
                        P.op("dve", "reciprocal", r1_, O1[:, 128:129], reads=rk, writes=[("rcp", 1)])
                        P.op("dve", "tensor_tensor", r1_, r1_, lam_ap[0:npart, :], ALU.mult,
                             reads=[("rcp", 1), ("lam",)], writes=[("rcp", 1)])
                        P.op("dve", "tensor_scalar", t1[0:npart, :], O1[:, 0:128], r1_, None, ALU.mult,
                             reads=rk + [("rcp", 1)], writes=[("t1",)])
                        P.op("dve", "scalar_tensor_tensor", odt[si_][0:npart, :], O0[:, 0:128], r0,
                             t1[0:npart, :], ALU.mult, ALU.subtract,
                             reads=rk + [("rcp", 0), ("t1",)], writes=[("od", id(odt), si_)])
                        P.op("dve", "scalar_tensor_tensor", sq[0:npart, :], odt[si_][0:npart, :], 1.0,
                             odt[si_][0:npart, :], ALU.mult, ALU.mult,
                             accum_out=den[0:npart, d0 + si_:d0 + si_ + 1],
                             reads=[("od", id(odt), si_)], writes=[("sq",), dk + (si_,)])

            def s23():
                if fox:
                    return
                P.op("act", "activation", den[0:npart, d0 + 4:d0 + 4 + n], den[0:npart, d0:d0 + n], AF.Ln,
                     scale=1.0 / 128, bias=epsc[0:npart, 0:1],
                     reads=[dk + (i_,) for i_ in range(n)] + [("epsc",)], writes=[dk + ("r",)])
                P.op("act", "activation", den[0:npart, d0 + 4:d0 + 4 + n], den[0:npart, d0 + 4:d0 + 4 + n], AF.Exp,
                     scale=-0.5, reads=[dk + ("r",)], writes=[dk + ("r",)])
                for si_, (Oreg, out_ap, slot) in enumerate(items):
                    P.op("dve", "scalar_tensor_tensor", out_ap, odt[si_][0:npart, :],
                         den[0:npart, d0 + 4 + si_:d0 + 5 + si_], g8[0:npart, :], ALU.mult, ALU.mult,
                         reads=[("od", id(odt), si_), dk + ("r",), ("g8",)], writes=[slot + (0,), slot + (1,)])

            return s1, s23

        def load_weights(u):
            sl = u % 2
            P.dma("pool", wqkb[sl][:, :, :], wqk[u].rearrange("p (k j) -> p k j", k=8), writes=[("wqkb", sl)])
            P.dma("pool", wkvb[sl][:, :, :], wkv[u].rearrange("p (k j) -> p k j", k=8), writes=[("wkvb", sl)])
            P.dma("pool", wob[u % 4][:, :], wo[u], writes=[("wob", u % 4)])

        def load_sample(sj):
            u, b = divmod(sj, 4)
            fox = u < 4
            cs_ = sj % 2
            kc_, vc_ = kTc[cs_], Vc[cs_]
            kkc, kvc = ("kTc", cs_), ("Vc", cs_)
            for m in range(2):
                P.dma("pool", kc_[0:64, m, :], kTc_d[u, b, 64 * m:64 * m + 64, :], writes=[kkc])
            if fox:
                if sj < 2:
                    P.op("pool", "memset", kc_[64:70, :, :], 1.0, writes=[kkc])
                    P.op("pool", "memset", vc_[:, :, 64:65], 1.0, writes=[kvc])
                    P.op("pool", "memset", vc_[:, :, 130:131], 1.0, writes=[kvc])
                for m in range(2):
                    P.dma("sp", kc_[67:70, m, :], caugS[1, b, 2 * u + m, :, 0:TP], reads=[("caug", 0), ("caug", 1), ("caug", 2), ("caug", 3)], writes=[kkc])
                P.dma("pool", vc_[:, :, :].rearrange("p j (m c) -> p j m c", m=2)[:, :, :, 0:64],
                      vc_d[u, b].rearrange("p j (m d) -> p j m d", m=2), writes=[kvc])
            else:
                for m in range(2):
                    P.dma("pool", kc_[64:70, m, :], alk_d[u - 4, :, 0:TP], writes=[kkc])
                P.dma("pool", vc_[:, :, 0:128], vc_d[u, b], writes=[kvc])
                if sj < 18:
                    P.op("pool", "memset", vc_[:, :, 128:129], 1.0, writes=[kvc])

        if DBG.get("nsb", 4) == 4:
            load_sample(0)
            load_sample(1)

        for u in DBG.get("units", range(8)):
            fox = u < 4
            W = 65 if fox else 129
            sl = u % 2
            q_, k_, V_, Vn_ = qT[0], kT[0], Vt[0], Vn[0]
            wq_, wk_ = wqkb[sl], wkvb[sl]
            kq, kk, kV, kVn = ("qT", 0), ("kT", 0), ("Vt", 0), ("Vn", 0)
            kqa = [("qTa", i_) for i_ in range(4)]
            kka = [("kTa", i_) for i_ in range(4)]
            vall = [("Vt", t_, m_) for t_ in range(16) for m_ in range(2)]

            def vcols(m):
                return (66 * m, 65) if fox else (0, 129)

            if u == 0:
                load_weights(0)
            if "aug" in DBG.get("skip", ()):
                pass
            elif fox:
                if u == 0:
                    P.op("pool", "memset", q_[64:70, :, :], 1.0, writes=kqa)
                    P.op("pool", "memset", k_[64:70, :, :], 1.0, writes=kka)
                for m in range(2):
                    h = 2 * u + m
                    P.dma("sp", q_[64:67, m, 0:TP], caugP[0, h], reads=[("caug", 0), ("caug", 1), ("caug", 2), ("caug", 3)], writes=[kqa[m]])
                    P.dma("sp", k_[67:70, m, 0:TP], caugP[1, h], reads=[("caug", 0), ("caug", 1), ("caug", 2), ("caug", 3)], writes=[kka[m]])
                    P.dma("sp", q_[64:67, m, TP:T].rearrange("r (b s) -> r b s", b=4),
                          caugS[0, :, h, :, TP:TP + 32].rearrange("b r s -> r b s"), reads=[("caug", 0), ("caug", 1), ("caug", 2), ("caug", 3)], writes=[kqa[2 + m]])
                    P.dma("sp", k_[67:70, m, TP:T].rearrange("r (b s) -> r b s", b=4),
                          caugS[1, :, h, :, TP:TP + 32].rearrange("b r s -> r b s"), reads=[("caug", 0), ("caug", 1), ("caug", 2), ("caug", 3)], writes=[kka[2 + m]])
                if u == 0:
                    P.op("pool", "memset", V_[:, :, 64:65], 1.0, writes=vall)
                    P.op("pool", "memset", V_[:, :, 130:131], 1.0, writes=vall)
                    P.op("pool", "memset", Vn_[:, :, 64:65], 1.0, writes=[kVn])
                    P.op("pool", "memset", Vn_[:, :, 130:131], 1.0, writes=[kVn])
            else:
                h = u - 4
                for m in range(2):
                    P.dma("pool", q_[64:70, m, :], alq_d[h], writes=[kqa[m], kqa[2 + m]])
                    P.dma("pool", k_[64:70, m, :], alk_d[h], writes=[kka[m], kka[2 + m]])
                if u == 4:
                    P.op("pool", "memset", V_[:, :, 128:129], 1.0, writes=vall)
                    P.op("pool", "memset", Vn_[:, :, 128:129], 1.0, writes=[kVn])

            def fm_block(ci, dst, dkey, scl, s0, sn):
                bk = sbank[0] % 2
                sbank[0] += 1
                ps = PS[bk]
                hk = [kk_ for t in tiles_of(s0, sn) for kk_ in hTk(t)]
                for k in range(8):
                    P.op("pe", "matmul", ps[:, 0:sn], wq_[:, k, ci * 128:(ci + 1) * 128], hT[:, k, s0:s0 + sn],
                         start=(k == 0), stop=(k == 7), reads=[("wqkb", sl)] + hk, writes=[("ps", bk)])
                P.op("act", "activation", dst[0:64, 0, s0:s0 + sn], ps[0:64, 0:sn], AF.Copy, scale=scl,
                     reads=[("ps", bk)], writes=[dkey])
                P.op("dve", "tensor_scalar", dst[0:64, 1, s0:s0 + sn], ps[64:128, 0:sn], scl, None, ALU.mult,
                     reads=[("ps", bk)], writes=[dkey])

            def tm_tile(t):
                bk = 2 + t % 2
                ps = PS[bk]
                for k in range(8):
                    P.op("pe", "matmul", ps[:, 0:256], hT[:, k, t * 128:(t + 1) * 128], wk_[:, k, :],
                         start=(k == 0), stop=(k == 7), reads=hTk(t) + [("wkvb", sl)], writes=[("ps", bk)])
                si = kvcnt[0] % 3
                kvcnt[0] += 1
                P.op("act", "activation", kvst[si][:, :], ps[:, 0:256], AF.Copy,
                     reads=[("ps", bk)], writes=[("kvst", si)])
                P.dma("sp", kv_d[u, t * 128:(t + 1) * 128, :], kvst[si][:, :], reads=[("kvst", si)])
                if t < 16:
                    if fox:
                        for m in range(2):
                            P.op("act", "activation", V_[:, t, 66 * m:66 * m + 64], ps[:, 128 + 64 * m:192 + 64 * m],
                                 AF.Copy, reads=[("ps", bk)], writes=[(kV[0], t, m)])
                    else:
                        P.op("act", "activation", V_[:, t, 0:128], ps[:, 128:256], AF.Copy,
                             reads=[("ps", bk)], writes=[(kV[0], t, 0), (kV[0], t, 1)])

            fm_list = [(ci, dst, dkey, scl, s0, sn) for ci, (dst, dkey, scl) in enumerate(((q_, kq, 0.125), (k_, kk, 1.0)))
                       for (s0, sn) in SUBS]
            tms = list(range(NT))
            for fi, fa_ in enumerate(fm_list):
                fm_block(*fa_)
                for _ in range(2 if fi < 7 else 1):
                    if tms:
                        tm_tile(tms.pop(0))
            while tms:
                tm_tile(tms.pop(0))
            for b in range(4 if "stm" not in DBG.get("skip", ()) else 0):
                bk = 2 + b % 2
                ps = PS[bk]
                c0 = TP + 32 * b
                for k in range(8):
                    P.op("pe", "matmul", ps[0:32, 0:256], hT[:, k, c0:c0 + 32], wk_[:, k, :],
                         start=(k == 0), stop=(k == 7), reads=hTk(16) + [("wkvb", sl)], writes=[("ps", bk)])
                if fox:
                    for m in range(2):
                        P.op("act", "activation", Vn_[:, b, 66 * m:66 * m + 64], ps[0:32, 128 + 64 * m:192 + 64 * m],
                             AF.Copy, reads=[("ps", bk)], writes=[kVn])
                else:
                    P.op("act", "activation", Vn_[:, b, 0:128], ps[0:32, 128:256], AF.Copy,
                         reads=[("ps", bk)], writes=[kVn])

            if u + 1 < 8:
                load_weights(u + 1)

            OSETS = [((PS[4], ("ps", 4)), (PS[5], ("ps", 5)), (PS[6], ("ps", 6))),
                     ((PS[2], ("ps", 2)), (PS[3], ("ps", 3)), (TRF, ("TR", 0)))]
            SBK = [0, 1, 6] if fox else [0, 1]

            def prompt_qt(qt, deferred):
                oset = OSETS[qt % 2]
                rk = [oset[0][1], oset[1][1]] + ([] if fox else [oset[2][1]])

                def Oreg_p(m, sub):
                    if fox:
                        return oset[m][0][:, sub * 66: sub * 66 + W]
                    if sub < 3:
                        return oset[m][0][:, sub * 130: sub * 130 + W]
                    return oset[2][0][:, m * 130: m * 130 + W]

                def okey(m, sub):
                    return oset[m][1] if (sub < 3 or fox) else oset[2][1]

                q0 = 512 * qt
                steps = [(m, j) for m in range(2) for j in range(4 * qt + 4)]
                pend = []
                started = set()

                def do_S(m, j):
                    jj = j - 4 * qt
                    qs = q0 + (128 * jj if jj > 0 else 0)
                    n = q0 + 512 - qs
                    bk = SBK[sbank[0] % len(SBK)]
                    sbank[0] += 1
                    ps = PS[bk]
                    diag = jj >= 0
                    P.op("pe", "matmul", ps[:, 0:n], k_[:, m, 128 * j:128 * j + 128], q_[:, m, qs:qs + n],
                         start=True, stop=not diag, reads=[kk, kq] + kqa + kka, writes=[("ps", bk)])
                    if diag:
                        di = 0 if fox else 1 + (u - 4)
                        P.op("pe", "matmul", ps[:, 0:128], identb[:, :], dm[:, di, :], start=False, stop=True,
                             reads=[("identb",), ("dm",)], writes=[("ps", bk)])
                    pi = ptc[0] % 4
                    ptc[0] += 1
                    P.op("act", "activation", pT[pi][:, 0:n], ps[:, 0:n], AF.Exp, reads=[("ps", bk)],
                         writes=[("pT", pi)])
                    return (m, j, pi, max(jj, 0))

                def do_PV(m, j, pi, jj0):
                    c0, cw = vcols(m)
                    for sub in range(jj0, 4):
                        ok = okey(m, sub)
                        first = ok not in started
                        started.add(ok)
                        last = (j == 4 * qt + sub)
                        P.op("pe", "matmul", Oreg_p(m, sub), pT[pi][:, (sub - jj0) * 128:(sub - jj0) * 128 + 128],
                             V_[:, j, c0:c0 + cw], start=first, stop=last, skip_group_check=True,
                             reads=[("pT", pi), ("Vt", j, 0), ("Vt", j, 1)], writes=[ok])

                for i_, st_ in enumerate(steps):
                    pend.append(do_S(*st_))
                    if len(pend) > len(SBK) - 1:
                        do_PV(*pend.pop(0))
                        for _d in range(NDUMMY):
                            P.op("pe", "matmul", PS[6][:, 260:512], identb[:, :],
                                 dm[:, 0:2, :].rearrange("p a b -> p (a b)")[:, 0:252],
                                 start=False, stop=False, skip_group_check=True, reads=[("identb",), ("dm",)])
                    if i_ == 3:
                        for f_ in deferred[0]:
                            f_()
                    if i_ == min(11, len(steps) - 1):
                        for f_ in deferred[1]:
                            f_()
                while pend:
                    do_PV(*pend.pop(0))

                items = [((lambda m, sub=sub: Oreg_p(m, sub)), ou[:, 4 * qt + sub, :], ("ou", 4 * qt + sub))
                         for sub in range(4)]
                return finalize(u, 128, items, rk, od)

            def sample_attn(b, osb):
                cs_ = (u * 4 + b) % 2
                kc_, vc_ = kTc[cs_], Vc[cs_]
                kkc, kvc = ("kTc", cs_), ("Vc", cs_)
                qc0 = TP + 32 * b
                OS = PS[osb]
                first = [True]

                def Oreg_s(m):
                    return OS[0:32, m * 130:m * 130 + W]

                def S_grp(g):
                    bk = sbank[0] % 2
                    sbank[0] += 1
                    ps = PS[bk]
                    for jj in range(4):
                        j = 4 * g + jj
                        for m in range(2):
                            cidx = (jj * 2 + m) * 32
                            P.op("pe", "matmul", ps[:, cidx:cidx + 32], kc_[:, m, 128 * j:128 * j + 128],
                                 q_[:, m, qc0:qc0 + 32], start=True, stop=True, skip_group_check=True,
                                 reads=[kkc, kq] + kqa, writes=[("ps", bk)])
                    pi = ptc[0] % 4
                    ptc[0] += 1
                    P.op("act", "activation", pT[pi][:, 0:256], ps[:, 0:256], AF.Exp, reads=[("ps", bk)],
                         writes=[("pT", pi)])
                    return pi

                def PV_grp(g, pi):
                    for jj in range(4):
                        j = 4 * g + jj
                        for m in range(2):
                            cidx = (jj * 2 + m) * 32
                            c0, cw = vcols(m)
                            P.op("pe", "matmul", Oreg_s(m), pT[pi][:, cidx:cidx + 32], vc_[:, j, c0:c0 + cw],
                                 start=first[0], stop=False, skip_group_check=True,
                                 reads=[("pT", pi), kvc], writes=[("ps", osb)])
                            first[0] = False

                def S_new():
                    bk = sbank[0] % 2
                    sbank[0] += 1
                    ps = PS[bk]
                    di = 0 if fox else 1 + (u - 4)
                    for m in range(2):
                        P.op("pe", "matmul", ps[0:32, m * 32:m * 32 + 32], k_[:, m, qc0:qc0 + 32],
                             q_[:, m, qc0:qc0 + 32], start=True, stop=False, skip_group_check=True,
                             reads=[kk, kq] + kqa + kka, writes=[("ps", bk)])
                        P.op("pe", "matmul", ps[0:32, m * 32:m * 32 + 32], identb[0:32, 0:32], dm[0:32, di, 0:32],
                             start=False, stop=True, skip_group_check=True,
                             reads=[("identb",), ("dm",)], writes=[("ps", bk)])
                    pi = ptc[0] % 4
                    ptc[0] += 1
                    P.op("act", "activation", pT[pi][0:32, 0:64], ps[0:32, 0:64], AF.Exp, reads=[("ps", bk)],
                         writes=[("pT", pi)])
                    return pi

                def PV_new(pi):
                    for m in range(2):
                        c0, cw = vcols(m)
                        P.op("pe", "matmul", Oreg_s(m), pT[pi][0:32, m * 32:m * 32 + 32], Vn_[0:32, b, c0:c0 + cw],
                             start=False, stop=True, skip_group_check=True,
                             reads=[("pT", pi), kVn], writes=[("ps", osb)])

                pis = [S_grp(0)]
                for g in range(1, 4):
                    pis.append(S_grp(g))
                    PV_grp(g - 1, pis[g - 1])
                pn = S_new()
                PV_grp(3, pis[3])
                PV_new(pn)
                return finalize(u, 32, [(Oreg_s, oS[0:32, b, :], ("oS", b))], [("ps", osb)], ods)


            dfr = ([], [])
            for qt in range(4):
                nxt = ([], [])
                if qt < DBG.get("nqt", 4):
                    p1, p23 = prompt_qt(qt, dfr)
                    nxt[0].append(p1)
                    nxt[1].append(p23)
                else:
                    for f_ in dfr[0] + dfr[1]:
                        f_()
                if qt < DBG.get("nsb", 4):
                    s1_, s23_ = sample_attn(qt, 2 if qt % 2 == 0 else 4)
                    s1_()
                    nxt[1].append(s23_)
                    if DBG.get("nsb", 4) == 4 and u * 4 + qt + 2 < 32:
                        load_sample(u * 4 + qt + 2)
                dfr = nxt
            for f_ in dfr[0] + dfr[1]:
                f_()
            if DBG.get("no_out"):
                continue
            uu = u % 2
            for g in range(5):
                trv, trk = TRU[g % 2]
                if g < 4:
                    for jj in range(4):
                        t = g * 4 + jj
                        P.op("pe", "transpose", trv[:, jj * 128:(jj + 1) * 128], ou[:, t, :], identb[:, :],
                             reads=[("ou", t, 0), ("ou", t, 1), ("identb",)], writes=[trk])
                    P.op("act", "activation", oT[:, uu, g * 512:(g + 1) * 512], trv[:, 0:512], AF.Copy,
                         reads=[trk], writes=[("oT", uu, g)])
                else:
                    for b in range(4):
                        P.op("pe", "transpose", trv[:, b * 32:(b + 1) * 32], oS[0:32, b, :], identb[0:32, 0:32],
                             reads=[("oS", b, 0), ("oS", b, 1), ("identb",)], writes=[trk])
                    P.op("act", "activation", oT[:, uu, TP:T], trv[:, 0:128], AF.Copy,
                         reads=[trk], writes=[("oT", uu, 4)])
            if uu == 1:
                for t in range(NT):
                    for dh in range(2):
                        bk = 2 + pocnt[0] % 2
                        pocnt[0] += 1
                        po = PS[bk]
                        for w in range(2):
                            wsl = (u - 1 + w) % 4
                            P.op("pe", "matmul", po[:, :], oT[:, w, t * 128:(t + 1) * 128],
                                 wob[wsl][:, dh * 512:(dh + 1) * 512], start=(w == 0), stop=(w == 1),
                                 reads=[("oT", w, t // 4), ("wob", wsl)], writes=[("ps", bk)])
                        P.op("dve", "tensor_tensor", x[:, t, dh * 512:(dh + 1) * 512], po[:, :],
                             x[:, t, dh * 512:(dh + 1) * 512], ALU.add,
                             reads=[("ps", bk), ("x", t)], writes=[("x", t)])

        stk[0].close()
        P.realloc(U_NAMES, old_names, fscr[0:1, 1:2])


    if stage >= 2:
        phase_c()
    if stage >= 3:
        phase_u()
    elif stage >= 2:
        stk[0].close()
        P.realloc(C_NAMES, old_names, fscr[0:1, 1:2])

    if stage >= 4:
        ffn(w2a, w2b, 2, "b")

    gfin = nc.alloc_sbuf_tensor("s_gfin", [128, DM], F32)
    P.realloc(old_names, ["gfin"], fscr[0:1, 3:4])
    P.dma("sp", gfin[:, :], gfin_d[0:1, :].broadcast_to([128, DM]), writes=[("gfin",)])
    y_v = y_d.rearrange("(j p) d -> p j d", p=128)
    rstd_all()
    for t in range(NT):
        P.op("dve", "scalar_tensor_tensor", x[:, t, :], x[:, t, :], rstd[:, t:t + 1], gfin[:, :], ALU.mult, ALU.mult,
             reads=[("x", t), ("rstd",), ("gfin",)], writes=[("x", t)])
        P.dma("sp", y_v[:, t, :], x[:, t, :], reads=[("x", t)])

    P.finalize_and_emit()
    return nc


def _cols2tile(w):
    n = w.shape[1]
    return np.ascontiguousarray(w.reshape(8, 128, n).transpose(1, 0, 2))


def _ffn_a(w):
    G = w[:, :DFF].reshape(8, 128, NCH, 128).transpose(2, 1, 0, 3)
    U = w[:, DFF:].reshape(8, 128, NCH, 128).transpose(2, 1, 0, 3)
    return np.ascontiguousarray(np.concatenate([G, U], axis=-1)).reshape(NCH, 128, 2048)


def _ffn_b(w):
    return np.ascontiguousarray(w.reshape(NCH, 128, 1024).transpose(1, 0, 2))


def _const_tables():
    pos = np.concatenate([np.arange(TP), TP + (np.arange(T - TP) % 32)]).astype(np.float32)
    alq = np.zeros((4, 6, T), np.float32)
    alk = np.zeros((4, 6, T), np.float32)
    for h, m in enumerate(SLOPES):
        alq[h, 0] = -m * 64.0 * np.floor(pos / 64.0)
        alq[h, 1] = -m * np.mod(pos, 64.0)
        alq[h, 2] = 1.0
        alq[h, 3] = 1.0
        alk[h, 0] = 1.0
        alk[h, 1] = 1.0
        alk[h, 2] = m * 64.0 * np.floor(pos / 64.0)
        alk[h, 3] = m * np.mod(pos, 64.0)
    s = np.arange(128)[:, None]
    t = np.arange(128)[None, :]
    dmask = np.zeros((128, 5, 128), np.float32)
    dmask[:, 0, :] = np.where(s > t, NEG, 0.0)
    for h, m in enumerate(SLOPES):
        dmask[:, 1 + h, :] = np.where(s > t, -2.0 * m * (s - t), 0.0) + np.where(s // 64 > t // 64, NEG, 0.0)
    return alq, alk, dmask


_NC_CACHE = {}


def kernel(x_prompt, x_sample, cache_fox_k, cache_fox_v, cache_fox_logf, cache_diff_k, cache_diff_v,
           norm_ffn1, w_ffn1_in, w_ffn1_out, norm_mix, w_in, b_forget,
           lambda_q1, lambda_k1, lambda_q2, lambda_k2, diff_subln, w_out,
           norm_ffn2, w_ffn2_in, w_ffn2_out, norm_final):
    f = lambda a: np.asarray(a, dtype=np.float32)
    x_prompt, x_sample = f(x_prompt), f(x_sample)
    cache_fox_k, cache_fox_v, cache_fox_logf = f(cache_fox_k), f(cache_fox_v), f(cache_fox_logf)
    cache_diff_k, cache_diff_v = f(cache_diff_k), f(cache_diff_v)
    w_in0 = f(w_in)[0]
    w_out0 = f(w_out)[0]
    shared = {}
    shared["w1a"] = _ffn_a(f(w_ffn1_in)[0])
    shared["w1b"] = _ffn_b(f(w_ffn1_out)[0])
    shared["w2a"] = _ffn_a(f(w_ffn2_in)[0])
    shared["w2b"] = _ffn_b(f(w_ffn2_out)[0])
    qa, ka, va = w_in0[:, 0:512], w_in0[:, 512:1024], w_in0[:, 1024:1536]
    fa = w_in0[:, 1536:1544]
    qb, kb, vb = w_in0[:, 1544:2056], w_in0[:, 2056:2568], w_in0[:, 2568:3080]
    wqk = np.zeros((8, 128, 8, 256), np.float32)
    wkv = np.zeros((8, 128, 8, 256), np.float32)
    for u in range(8):
        q_, k_, v_ = (qa, ka, va) if u < 4 else (qb, kb, vb)
        c = 128 * (u % 4)
        wqk[u, :, :, 0:128] = _cols2tile(q_[:, c:c + 128])
        wqk[u, :, :, 128:256] = _cols2tile(k_[:, c:c + 128])
        wkv[u, :, :, 0:128] = _cols2tile(k_[:, c:c + 128])
        wkv[u, :, :, 128:256] = _cols2tile(v_[:, c:c + 128])
    shared["wqk"] = wqk.reshape(8, 128, 2048)
    shared["wkv"] = wkv.reshape(8, 128, 2048)
    shared["wfa"] = _cols2tile(fa).reshape(128, 64)
    shared["wo"] = np.ascontiguousarray(w_out0.reshape(8, 128, 1024))
    gc = np.stack([f(norm_ffn1)[0], f(norm_mix)[0], f(norm_ffn2)[0]], 0)
    shared["gvec"] = np.ascontiguousarray(gc)
    shared["gcols"] = np.ascontiguousarray(gc.reshape(3, 8, 128).transpose(2, 0, 1)).reshape(128, 24)
    shared["gfin"] = f(norm_final).reshape(1, DM)
    shared["bfg"] = np.ascontiguousarray(np.tile(f(b_forget)[0], NT)).reshape(1, NT * 8)
    shared["lamv"] = np.concatenate([f(lambda_q1)[0], f(lambda_k1)[0], f(lambda_q2)[0], f(lambda_k2)[0]]).reshape(1, 256)
    shared["subln"] = f(diff_subln).reshape(1, 128)
    shared["ident"] = np.eye(128, dtype=np.float32)
    alq, alk, dmask = _const_tables()
    shared["alq"], shared["alk"], shared["dmask"] = alq, alk, dmask

    in_maps = []
    for i in range(8):
        b0 = 4 * i
        m = dict(shared)
        m["xin"] = np.ascontiguousarray(np.concatenate([x_prompt[i], x_sample[b0:b0 + 4].reshape(128, DM)], 0))
        fk = cache_fox_k[0, b0:b0 + 4].reshape(4, TP, 4, 128).transpose(2, 0, 3, 1)
        dk = cache_diff_k[0, b0:b0 + 4].reshape(4, TP, 4, 128).transpose(2, 0, 3, 1)
        m["kTc"] = np.ascontiguousarray(np.concatenate([fk, dk], 0))
        fv = cache_fox_v[0, b0:b0 + 4].reshape(4, 16, 128, 4, 128).transpose(3, 0, 2, 1, 4)
        dv = cache_diff_v[0, b0:b0 + 4].reshape(4, 16, 128, 4, 128).transpose(3, 0, 2, 1, 4)
        m["vc"] = np.ascontiguousarray(np.concatenate([fv, dv], 0))
        m["plf"] = np.ascontiguousarray(cache_fox_logf[0, b0:b0 + 4].transpose(0, 2, 1)).reshape(32, TP)
        in_maps.append(m)

    if "nc" not in _NC_CACHE:
        _NC_CACHE["nc"] = build_nc()
    nc = _NC_CACHE["nc"]
    res = run_bass_kernel_spmd(nc, in_maps, core_ids=list(range(8)))
    R = res.results

    y_prompt = np.zeros((8, TP, DM), np.float32)
    y_sample = np.zeros((32, 32, DM), np.float32)
    nfk_p = np.zeros((1, 8, TP, 8, 64), np.float32)
    nfv_p = np.zeros((1, 8, TP, 8, 64), np.float32)
    nlf_p = np.zeros((1, 8, TP, 8), np.float32)
    ndk_p = np.zeros((1, 8, TP, 4, 2, 64), np.float32)
    ndv_p = np.zeros((1, 8, TP, 4, 128), np.float32)
    nfk_s = np.zeros((1, 32, 32, 8, 64), np.float32)
    nfv_s = np.zeros((1, 32, 32, 8, 64), np.float32)
    nlf_s = np.zeros((1, 32, 32, 8), np.float32)
    ndk_s = np.zeros((1, 32, 32, 4, 2, 64), np.float32)
    ndv_s = np.zeros((1, 32, 32, 4, 128), np.float32)
    for i in range(8):
        b0 = 4 * i
        y = R[i]["y"]
        kv = R[i]["kv"]
        lfo = R[i]["lfo"]
        y_prompt[i] = y[:TP]
        y_sample[b0:b0 + 4] = y[TP:].reshape(4, 32, DM)
        nlf_p[0, i] = lfo[:TP]
        nlf_s[0, b0:b0 + 4] = lfo[TP:].reshape(4, 32, 8)
        for u in range(4):
            nfk_p[0, i, :, 2 * u:2 * u + 2, :] = kv[u, :TP, 0:128].reshape(TP, 2, 64)
            nfv_p[0, i, :, 2 * u:2 * u + 2, :] = kv[u, :TP, 128:256].reshape(TP, 2, 64)
            nfk_s[0, b0:b0 + 4, :, 2 * u:2 * u + 2, :] = kv[u, TP:, 0:128].reshape(4, 32, 2, 64)
            nfv_s[0, b0:b0 + 4, :, 2 * u:2 * u + 2, :] = kv[u, TP:, 128:256].reshape(4, 32, 2, 64)
            ndk_p[0, i, :, u] = kv[4 + u, :TP, 0:128].reshape(TP, 2, 64)
            ndv_p[0, i, :, u] = kv[4 + u, :TP, 128:256]
            ndk_s[0, b0:b0 + 4, :, u] = kv[4 + u, TP:, 0:128].reshape(4, 32, 2, 64)
            ndv_s[0, b0:b0 + 4, :, u] = kv[4 + u, TP:, 128:256].reshape(4, 32, 128)
    return (y_prompt, y_sample, nfk_p, nfv_p, nlf_p, ndk_p, ndv_p, nfk_s, nfv_s, nlf_s, ndk_s, ndv_s)
```

```python
import contextlib
import numpy as np
import concourse.bass as bass
import concourse.mybir as mybir
from concourse.bass_utils import run_bass_kernel_spmd

F32 = mybir.dt.float32
BF16 = mybir.dt.bfloat16
AF = mybir.ActivationFunctionType
ALU = mybir.AluOpType
AX = mybir.AxisListType

NT = 17
T = 2176
TP = 2048
DM = 1024
DFF = 2816
NCH = 22
EPS = 1e-6
SUBS = [(0, 512), (512, 512), (1024, 512), (1536, 512), (2048, 128)]
QUARTERS = [(0, 6), (6, 6), (12, 5), (17, 5)]
SLOPES = (0.25, 0.0625, 0.015625, 0.00390625)
LAMBDA_INIT = 0.2
NEG = -30000.0
NDUMMY = 0
DBG = {}


class Op:
    __slots__ = ("eng", "fn", "deps", "is_dma", "sig", "idx", "dsem", "dval", "pdeps", "need")


class Prog:
    ENGS = ("pe", "act", "dve", "pool", "sp")
    SAME_ENGINE_SYNC = ("act", "dve", "pool")

    def __init__(self, nc, n_dma_sems=12):
        self.nc = nc
        self.ops = {e: [] for e in self.ENGS}
        self.last_w = {}
        self.readers = {}
        self.fences = {}
        self.n_dma_sems = n_dma_sems
        self.dma_count = {e: 0 for e in self.ENGS}
        self.all_dma = []

    def add(self, eng, fn, reads=(), writes=(), dma=False):
        op = Op()
        op.eng, op.fn, op.is_dma = eng, fn, dma
        op.sig = None
        op.need = False
        op.dsem = op.dval = None
        deps = set()
        for k in reads:
            w = self.last_w.get(k)
            if w is None:
                w = self.fences.get(k[0])
            if w is not None:
                deps.add(w)
        for k in writes:
            w = self.last_w.get(k)
            if w is None:
                w = self.fences.get(k[0])
            if w is not None:
                deps.add(w)
            for r in self.readers.get(k, ()):
                deps.add(r)
        op.deps = deps
        for k in writes:
            self.last_w[k] = op
            self.readers[k] = []
        for k in reads:
            self.readers.setdefault(k, []).append(op)
        op.idx = len(self.ops[eng])
        self.ops[eng].append(op)
        if dma:
            j = self.dma_count[eng]
            self.dma_count[eng] = j + 1
            op.dsem = (eng, j % self.n_dma_sems)
            op.dval = 16 * (j // self.n_dma_sems + 1)
            self.all_dma.append(op)
        return op

    def op(self, eng, method, *args, reads=(), writes=(), **kw):
        return self.add(eng, lambda e: getattr(e, method)(*args, **kw), reads, writes)

    def dma(self, q, out, in_, reads=(), writes=()):
        return self.add(q, lambda e: e.dma_start(out=out, in_=in_), reads, writes, dma=True)

    def realloc(self, old_names, new_names, scratch):
        old = set(old_names)
        keys = [k for k in set(self.last_w) | set(self.readers) if k[0] in old]
        f = self.add("pool", lambda e: e.memset(scratch, 0.0), reads=(), writes=keys + [("fence_scratch",)])
        for n in new_names:
            self.fences[n] = f
        for k in keys:
            self.last_w.pop(k, None)
            self.readers.pop(k, None)
        return f

    def finalize_and_emit(self):
        nc = self.nc
        for eng in self.ENGS:
            for op in self.ops[eng]:
                best = {}
                pd = []
                for d in op.deps:
                    if d.is_dma:
                        pd.append(d)
                    else:
                        if d.eng == eng and eng not in self.SAME_ENGINE_SYNC:
                            continue
                        b = best.get(d.eng)
                        if b is None or d.idx > b.idx:
                            best[d.eng] = d
                pd.extend(best.values())
                op.pdeps = pd
                for d in pd:
                    d.need = True
        for eng in self.ENGS:
            c = 0
            for op in self.ops[eng]:
                if op.need and not op.is_dma:
                    c += 1
                    op.sig = c
        with contextlib.ExitStack() as st:
            esem = {e: st.enter_context(nc.semaphore("s_" + e)) for e in ("pe", "act", "dve", "pool")}
            dsem = {}
            for e in self.ENGS:
                for i in range(min(self.n_dma_sems, self.dma_count[e])):
                    dsem[(e, i)] = st.enter_context(nc.semaphore("d_%s_%d" % (e, i)))
            block = st.enter_context(nc.Block())
            final_dma = {}
            for op in self.all_dma:
                final_dma[op.dsem] = max(final_dma.get(op.dsem, 0), op.dval)

            def emit(eng, e):
                waited = {}
                for op in self.ops[eng]:
                    for d in op.pdeps:
                        if d.is_dma:
                            key, val, sem = d.dsem, d.dval, dsem[d.dsem]
                        else:
                            key, val, sem = d.eng, d.sig, esem[d.eng]
                        if waited.get(key, 0) < val:
                            e.wait_ge(sem, val)
                            waited[key] = val
                    if op.is_dma and op.dval > 16 and waited.get(op.dsem, 0) < op.dval - 16:
                        e.wait_ge(dsem[op.dsem], op.dval - 16)
                        waited[op.dsem] = op.dval - 16
                    ins = op.fn(e)
                    if op.is_dma:
                        ins.then_inc(dsem[op.dsem], 16)
                    elif op.sig is not None:
                        ins.then_inc(esem[eng], 1)
                if eng == "sp":
                    for key, val in final_dma.items():
                        if waited.get(key, 0) < val:
                            e.wait_ge(dsem[key], val)

            @block.tensor
            def _(e):
                emit("pe", e)

            @block.scalar
            def _(e):
                emit("act", e)

            @block.vector
            def _(e):
                emit("dve", e)

            @block.gpsimd
            def _(e):
                emit("pool", e)

            @block.sync
            def _(e):
                emit("sp", e)


def build_nc(stage=99):
    nc = bass.Bass("TRN2", target_bir_lowering=False)

    def din(name, shape, dt=F32):
        return nc.dram_tensor(name, list(shape), dt, kind="ExternalInput").ap()

    def dout(name, shape):
        return nc.dram_tensor(name, list(shape), F32, kind="ExternalOutput").ap()

    xin = din("xin", [T, DM])
    w1a = din("w1a", [NCH, 128, 2048])
    w1b = din("w1b", [128, NCH, 1024])
    w2a = din("w2a", [NCH, 128, 2048])
    w2b = din("w2b", [128, NCH, 1024])
    wqk = din("wqk", [8, 128, 2048])
    wkv = din("wkv", [8, 128, 2048])
    wfa = din("wfa", [128, 64])
    wo = din("wo", [8, 128, 1024])
    gcols_d = din("gcols", [128, 24])
    gvec_d = din("gvec", [3, DM])
    gfin_d = din("gfin", [1, DM])
    bfg_d = din("bfg", [1, NT * 8])
    lam_d = din("lamv", [1, 256])
    subln_d = din("subln", [1, 128])
    ident_d = din("ident", [128, 128])
    dmask_d = din("dmask", [128, 5, 128])
    alq_d = din("alq", [4, 6, T])
    alk_d = din("alk", [4, 6, T])
    kTc_d = din("kTc", [8, 4, 128, TP])
    vc_d = din("vc", [8, 4, 128, 16, 128])
    plf_d = din("plf", [32, TP])
    y_d = dout("y", [T, DM])
    kv_d = dout("kv", [8, T, 256])
    lfo_d = dout("lfo", [T, 8])
    caugP = nc.dram_tensor("caugP", [2, 8, 3, TP], BF16, kind="Internal").ap()
    caugS = nc.dram_tensor("caugS", [2, 4, 8, 3, TP + 32], BF16, kind="Internal").ap()

    P = Prog(nc)

    def sb(name, shape, dt):
        return nc.alloc_sbuf_tensor("s_" + name, list(shape), dt)

    x = sb("x", [128, NT, DM], F32)
    hT = sb("hT", [128, 8, T], BF16)
    identb = sb("identb", [128, 128], BF16)
    identf = sb("identf", [128, 128], F32)
    dm = sb("dm", [128, 5, 128], BF16)
    gcols = sb("gcols", [128, 24], F32)
    ss = sb("ss", [128, NT], F32)
    rstd = sb("rstd", [128, NT], F32)
    hn = [sb("hn%d" % i, [128, DM], BF16) for i in range(2)]
    junk = sb("junk", [128, DM], BF16)
    fscr = sb("fscr", [128, 8], F32)
    epsc = sb("epsc", [128, 1], F32)
    PS = [nc.alloc_psum_tensor("ps%d" % i, [128, 512], F32) for i in range(7)]
    TR = nc.alloc_psum_tensor("trb", [128, 1024], BF16)

    xin_v = xin.rearrange("(j p) d -> p j d", p=128)
    for t in range(NT):
        P.dma("sp", x[:, t, :], xin_v[:, t, :], writes=[("x", t)])
    P.dma("pool", identb[:, :], ident_d[:, :], writes=[("identb",)])
    P.dma("sp", identf[:, :], ident_d[:, :], writes=[("identf",)])
    P.dma("pool", dm[:, :, :], dmask_d[:, :, :], writes=[("dm",)])
    P.dma("sp", gcols[:, :], gcols_d[:, :], writes=[("gcols",)])

    P.op("pool", "memset", epsc[:, :], EPS, writes=[("epsc",)])
    TRV = [(TR[:, :], ("TR", 0)), (PS[6][:, :].bitcast(BF16), ("ps", 6)), (PS[5][:, :].bitcast(BF16), ("ps", 5))]
    trc = [0]

    def rstd_all():
        for t in range(NT):
            P.op("act", "activation", junk[:, :], x[:, t, :], AF.Square, accum_out=ss[:, t:t + 1],
                 reads=[("x", t)], writes=[("junk",), ("ss",)])
        P.op("act", "activation", rstd[:, :], ss[:, :], AF.Ln, scale=1.0 / DM, bias=epsc[:, 0:1],
             reads=[("ss",), ("epsc",)], writes=[("rstd",)])
        P.op("act", "activation", rstd[:, :], rstd[:, :], AF.Exp, scale=-0.5,
             reads=[("rstd",)], writes=[("rstd",)])

    def norm_to_hT(gi, gt):
        P.dma("sp", gt[:, :], gvec_d[gi:gi + 1, :].broadcast_to([128, DM]), writes=[("gt",)])
        rstd_all()
        for t in range(NT):
            h = hn[t % 2]
            P.op("dve", "scalar_tensor_tensor", h[:, :], x[:, t, :], rstd[:, t:t + 1], gt[:, :], ALU.mult, ALU.mult,
                 reads=[("x", t), ("rstd",), ("gt",)], writes=[("hn", t % 2)])
            for half in range(2):
                trv, trk = TRV[trc[0] % 3]
                trc[0] += 1
                for kk in range(4):
                    k = half * 4 + kk
                    P.op("pe", "transpose", trv[:, kk * 128:(kk + 1) * 128], h[:, k * 128:(k + 1) * 128],
                         identb[:, :], reads=[("hn", t % 2), ("identb",)], writes=[trk])
                P.op("act", "activation", hT[:, half * 4:half * 4 + 4, t * 128:(t + 1) * 128],
                     trv[:, 0:512].rearrange("p (k c) -> p k c", k=4), AF.Copy,
                     reads=[trk], writes=[("hT", t, half * 4 + kk) for kk in range(4)])

    def hTk(t):
        return [("hT", t, k) for k in range(8)]

    def tiles_of(s0, sn):
        return list(range(s0 // 128, (s0 + sn) // 128))

    def ffn(wa, wb, gi, tagn):
        st = contextlib.ExitStack()
        actT = st.enter_context(nc.sbuf_tensor("s_actT" + tagn, [128, 6, T], BF16))
        wbq = [st.enter_context(nc.sbuf_tensor("s_wbq%s%d" % (tagn, i), [128, 6, 1024], BF16)) for i in range(2)]
        wab = [st.enter_context(nc.sbuf_tensor("s_wab%s%d" % (tagn, i), [128, 8, 256], BF16)) for i in range(3)]
        sg = [st.enter_context(nc.sbuf_tensor("s_sg%s%d" % (tagn, i), [128, 512], F32)) for i in range(2)]
        gt = st.enter_context(nc.sbuf_tensor("s_gt" + tagn, [128, DM], F32))
        names = ["actT", "wbq", "wab", "sg", "gt"]
        if not DBG.get("no_norm"):
            norm_to_hT(gi, gt)
        cnt = 0
        pcnt = 0
        for qi, (c0, nq) in enumerate(QUARTERS[:DBG.get("nq", 4)]):
            wq_ = wbq[qi % 2]
            P.dma("pool", wq_[:, 0:nq, :], wb[:, c0:c0 + nq, :], writes=[("wbq", qi % 2)])
            for cl in range(nq):
                c = c0 + cl
                wbuf = wab[c % 3]
                P.dma("pool", wbuf[:, :, :], wa[c].rearrange("p (k j) -> p k j", k=8), writes=[("wab", c % 3)])
                for (s0, sn) in SUBS:
                    pg = PS[cnt % 2]
                    pu = PS[2 + cnt % 2]
                    hk = [kk_ for t in tiles_of(s0, sn) for kk_ in hTk(t)]
                    for k in range(8):
                        P.op("pe", "matmul", pg[:, 0:sn], wbuf[:, k, 0:128], hT[:, k, s0:s0 + sn],
                             start=(k == 0), stop=(k == 7),
                             reads=[("wab", c % 3)] + hk, writes=[("ps", cnt % 2)])
                    for k in range(8):
                        P.op("pe", "matmul", pu[:, 0:sn], wbuf[:, k, 128:256], hT[:, k, s0:s0 + sn],
                             start=(k == 0), stop=(k == 7),
                             reads=[("wab", c % 3)] + hk, writes=[("ps", 2 + cnt % 2)])
                    s_ = sg[cnt % 2]
                    P.op("act", "activation", s_[:, 0:sn], pg[:, 0:sn], AF.Silu,
                         reads=[("ps", cnt % 2)], writes=[("sg", cnt % 2)])
                    P.op("dve", "tensor_tensor", actT[:, cl, s0:s0 + sn], s_[:, 0:sn], pu[:, 0:sn], ALU.mult,
                         reads=[("sg", cnt % 2), ("ps", 2 + cnt % 2)],
                         writes=[("actT", cl, t) for t in tiles_of(s0, sn)])
                    cnt += 1
            for t in range(NT if DBG.get("phaseB", 1) else 0):
                for dh in range(2):
                    po = PS[4 + pcnt % 2]
                    for cl in range(nq):
                        P.op("pe", "matmul", po[:, :], actT[:, cl, t * 128:(t + 1) * 128],
                             wq_[:, cl, dh * 512:(dh + 1) * 512], start=(cl == 0), stop=(cl == nq - 1),
                             reads=[("actT", cl, t), ("wbq", qi % 2)], writes=[("ps", 4 + pcnt % 2)])
                    P.op("dve", "scalar_tensor_tensor", x[:, t, dh * 512:(dh + 1) * 512], po[:, :], 0.5,
                         x[:, t, dh * 512:(dh + 1) * 512], ALU.mult, ALU.add,
                         reads=[("ps", 4 + pcnt % 2), ("x", t)], writes=[("x", t)])
                    pcnt += 1
        st.close()
        return names

    old_names = ["actT", "wbq", "wab", "sg", "gt"]
    if stage >= 1:
        ffn(w1a, w1b, 0, "a")

    C_NAMES = ["gt", "wfab", "lf", "lfw", "bft", "lfT", "cs", "cc", "r1", "ones_b", "c3", "c3n"]
    U_NAMES = ["qT", "kT", "Vt", "Vn", "kTc", "Vc", "wqkb", "wkvb", "wob", "pT", "kvst", "ou", "oS", "oT",
               "den", "t1", "od", "sq", "rcp", "g8", "lamt", "lamw", "lam"]
    stk = [None]

    def sbp(name, shape, dt):
        return stk[0].enter_context(nc.sbuf_tensor("s_" + name, list(shape), dt))

    def phase_c():
        st = contextlib.ExitStack()
        stk[0] = st

        wfab = sbp("wfab", [128, 8, 8], BF16)
        lf = sbp("lf", [128, NT * 8], F32)
        lfw = [sbp("lfw%d" % i, [128, NT * 8], F32) for i in range(3)]
        bft = sbp("bft", [128, NT * 8], F32)
        lfT = sbp("lfT", [8, T], F32)
        cs = sbp("cs", [40, TP + 32], F32)
        cc = sbp("cc", [40, TP + 32], F32)
        r1 = sbp("r1", [40, TP + 32], F32)
        ones_b = sbp("ones_b", [40, TP + 32], BF16)
        c3 = sbp("c3", [40, 3, TP + 32], BF16)
        c3n = sbp("c3n", [40, 3, TP + 32], BF16)
        c_names = ["wfab", "lf", "lfw", "bft", "lfT", "cs", "cc", "r1", "ones_b", "c3", "c3n"]
        P.realloc(old_names, C_NAMES, fscr[0:1, 0:1])
        P.dma("sp", cs[0:32, 0:TP], plf_d[:, :], writes=[("cs",)])

        gtc = sbp("gtc", [128, DM], F32)
        norm_to_hT(1, gtc)

        P.dma("sp", bft[:, :], bfg_d[0:1, :].broadcast_to([128, NT * 8]), writes=[("bft",)])
        P.dma("pool", wfab[:, :, :], wfa.rearrange("p (k j) -> p k j", k=8), writes=[("wfab",)])
        pf = PS[3]
        for t in range(NT):
            for k in range(8):
                P.op("pe", "matmul", pf[:, t * 8:(t + 1) * 8], hT[:, k, t * 128:(t + 1) * 128], wfab[:, k, :],
                     start=(k == 0), stop=(k == 7), skip_group_check=True,
                     reads=hTk(t) + [("wfab",)], writes=[("ps", 3)])
        NF = NT * 8
        P.op("dve", "tensor_tensor", lfw[0][:, :], pf[:, 0:NF], bft[:, :], ALU.add,
             reads=[("ps", 3), ("bft",)], writes=[("lfw", 0)])
        P.op("act", "activation", lfw[1][:, :], lfw[0][:, :], AF.Abs,
             reads=[("lfw", 0)], writes=[("lfw", 1)])
        P.op("act", "activation", lfw[1][:, :], lfw[1][:, :], AF.Exp, scale=-1.0,
             reads=[("lfw", 1)], writes=[("lfw", 1)])
        P.op("act", "activation", lfw[1][:, :], lfw[1][:, :], AF.Ln, bias=1.0,
             reads=[("lfw", 1)], writes=[("lfw", 1)])
        P.op("dve", "tensor_scalar", lfw[2][:, :], lfw[0][:, :], 0.0, None, ALU.min,
             reads=[("lfw", 0)], writes=[("lfw", 2)])
        P.op("dve", "tensor_tensor", lf[:, :], lfw[2][:, :], lfw[1][:, :], ALU.subtract,
             reads=[("lfw", 1), ("lfw", 2)], writes=[("lf",)])
        P.dma("sp", lfo_d.rearrange("(j p) h -> p j h", p=128), lf[:, :].rearrange("p (j h) -> p j h", h=8),
              reads=[("lf",)])
        P.op("pool", "memset", cs[32:40, TP:TP + 32], 0.0, writes=[("cs", "pz")])
        for g in range(5):
            n = 4 if g < 4 else 1
            for jj in range(n):
                t = g * 4 + jj
                P.op("pe", "transpose", PS[3][0:8, jj * 128:(jj + 1) * 128], lf[:, t * 8:(t + 1) * 8], identf[:, :],
                     reads=[("lf",), ("identf",)], writes=[("ps", 3)])
            if g < 4:
                P.op("dve", "tensor_copy", cs[32:40, g * 512:(g + 1) * 512], PS[3][0:8, 0:512],
                     reads=[("ps", 3)], writes=[("cs", "p", g)])
            else:
                P.op("dve", "tensor_copy", lfT[:, TP:T], PS[3][0:8, 0:128],
                     reads=[("ps", 3)], writes=[("lfT",)])
        P.op("pool", "memset", ones_b[:, :], 1.0, writes=[("ones_b",)])
        for b in range(4):
            P.dma("sp", cs[8 * b:8 * b + 8, TP:TP + 32], lfT[0:8, TP + 32 * b: TP + 32 * b + 32],
                  reads=[("lfT",)], writes=[("cs", "s", b)])
        P.op("dve", "tensor_tensor_scan", cc[:, :], ones_b[:, :], cs[:, :], 0.0, ALU.mult, ALU.add,
             reads=[("cs",), ("cs", "pz"), ("ones_b",)] + [("cs", "p", g) for g in range(4)]
             + [("cs", "s", b) for b in range(4)], writes=[("cc",)])
        n = TP + 32
        P.op("dve", "tensor_copy", c3[:, 0, :], cc[:, :], reads=[("cc",)], writes=[("c3",)])
        P.op("dve", "tensor_tensor", r1[:, :], cc[:, :], c3[:, 0, :], ALU.subtract,
             reads=[("cc",), ("c3",)], writes=[("r1",)])
        P.op("dve", "tensor_copy", c3[:, 1, :], r1[:, :], reads=[("r1",)], writes=[("c3",)])
        P.op("dve", "tensor_tensor", r1[:, :], r1[:, :], c3[:, 1, :], ALU.subtract,
             reads=[("r1",), ("c3",)], writes=[("r1",)])
        P.op("dve", "tensor_copy", c3[:, 2, :], r1[:, :], reads=[("r1",)], writes=[("c3",)])
        P.op("dve", "tensor_scalar", c3n[:, :, :], c3[:, :, :], -1.0, None, ALU.mult,
             reads=[("c3",)], writes=[("c3n",)])
        P.dma("sp", caugP[0], c3[32:40, :, 0:TP], reads=[("c3",)], writes=[("caug", 0)])
        P.dma("sp", caugP[1], c3n[32:40, :, 0:TP], reads=[("c3n",)], writes=[("caug", 1)])
        P.dma("sp", caugS[0].rearrange("b h r s -> (b h) r s"), c3[0:32, :, :], reads=[("c3",)], writes=[("caug", 2)])
        P.dma("sp", caugS[1].rearrange("b h r s -> (b h) r s"), c3n[0:32, :, :], reads=[("c3n",)], writes=[("caug", 3)])

    def phase_u():
        stk[0].close()
        st = contextlib.ExitStack()
        stk[0] = st
        qT = [sbp("qT%d" % i, [70, 2, T], BF16) for i in range(1)]
        kT = [sbp("kT%d" % i, [70, 2, T], BF16) for i in range(1)]
        Vt = [sbp("Vt%d" % i, [128, 16, 132], BF16) for i in range(1)]
        Vn = [sbp("Vn%d" % i, [32, 4, 132], BF16) for i in range(1)]
        kTc = [sbp("kTc%d" % i, [70, 2, TP], BF16) for i in range(2)]
        Vc = [sbp("Vc%d" % i, [128, 16, 132], BF16) for i in range(2)]
        wqkb = [sbp("wqkb%d" % i, [128, 8, 256], BF16) for i in range(2)]
        wkvb = [sbp("wkvb%d" % i, [128, 8, 256], BF16) for i in range(2)]
        wob = [sbp("wob%d" % i, [128, 1024], BF16) for i in range(4)]
        pT = [sbp("pT%d" % i, [128, 512], BF16) for i in range(4)]
        kvst = [sbp("kvst%d" % i, [128, 256], F32) for i in range(3)]
        ou = sbp("ou", [128, 16, 128], BF16)
        oS = sbp("oS", [32, 4, 128], BF16)
        oT = sbp("oT", [128, 2, T], BF16)
        den = sbp("den", [128, 64], F32)
        t1 = sbp("t1", [128, 128], F32)
        od = [sbp("od_%d" % i, [128, 128], F32) for i in range(4)]
        ods = [sbp("ods", [32, 128], F32)]
        sq = sbp("sqj", [128, 128], F32)
        rcp = sbp("rcp", [128, 2], F32)
        g8 = sbp("g8", [128, 128], F32)
        lamt = oT[:, 0, 0:512].bitcast(F32)
        lamw = sbp("lamw", [128, 8], F32)
        new_names = ["qT", "kT", "Vt", "Vn", "kTc", "Vc", "wqkb", "wkvb", "wob", "pT", "kvst", "ou", "oS", "oT",
                     "den", "t1", "od", "sq", "rcp", "g8", "lamt", "lamw", "lam"]
        P.realloc(C_NAMES, U_NAMES, fscr[0:1, 2:3])

        P.dma("sp", lamt[:, :], lam_d[0:1, :].broadcast_to([128, 256]), writes=[("oT", 0, 0)])
        P.dma("sp", g8[:, :], subln_d[0:1, :].broadcast_to([128, 128]), writes=[("g8",)])
        P.op("dve", "tensor_scalar", g8[:, :], g8[:, :], 1.0 - LAMBDA_INIT, None, ALU.mult,
             reads=[("g8",)], writes=[("g8",)])
        P.op("dve", "tensor_tensor", lamt[:, 0:64], lamt[:, 0:64], lamt[:, 64:128], ALU.mult,
             reads=[("oT", 0, 0)], writes=[("oT", 0, 0)])
        P.op("dve", "tensor_tensor", lamt[:, 128:192], lamt[:, 128:192], lamt[:, 192:256], ALU.mult,
             reads=[("oT", 0, 0)], writes=[("oT", 0, 0)])
        P.op("dve", "reduce_sum", lamw[:, 0:1], lamt[:, 0:64], AX.X, reads=[("oT", 0, 0)], writes=[("lamw",)])
        P.op("dve", "reduce_sum", lamw[:, 1:2], lamt[:, 128:192], AX.X, reads=[("oT", 0, 0)], writes=[("lamw",)])
        P.op("act", "activation", lamw[:, 2:4], lamw[:, 0:2], AF.Exp, reads=[("lamw",)], writes=[("lamw",)])
        P.op("dve", "tensor_tensor", lamw[:, 4:5], lamw[:, 2:3], lamw[:, 3:4], ALU.subtract,
             reads=[("lamw",)], writes=[("lamw",)])
        P.op("dve", "tensor_scalar", lamw[:, 5:6], lamw[:, 4:5], LAMBDA_INIT, None, ALU.add,
             reads=[("lamw",)], writes=[("lam",)])
        lam_ap = lamw[:, 5:6]


        kvcnt = [0]
        pending_out = []
        TRU = [(TR[:, :], ("TR", 0)), (PS[0][:, :].bitcast(BF16), ("ps", 0))]
        TRF = TR[:, :].bitcast(F32)
        ptc = [0]
        sbank = [0]
        pocnt = [0]

        dcnt = [0]

        def finalize(u, npart, items, rk, odt):
            fox = u < 4
            di_ = dcnt[0] % 8
            dcnt[0] += 1
            d0 = 8 * di_
            n = len(items)
            dk = ("den", di_)

            def s1():
                for si_, (Oreg, out_ap, slot) in enumerate(items):
                    if fox:
                        for m in range(2):
                            O = Oreg(m)
                            dkm = dk + (si_, m)
                            dn = den[0:npart, d0 + 2 * si_ + m:d0 + 2 * si_ + m + 1]
                            P.op("dve", "reciprocal", dn, O[:, 64:65], reads=rk, writes=[dkm])
                            P.op("dve", "tensor_scalar", out_ap[:, 64 * m:64 * m + 64], O[:, 0:64], dn, None,
                                 ALU.mult, reads=rk + [dkm], writes=[slot + (m,)])
                    else:
                        O0, O1 = Oreg(0), Oreg(1)
                        r0 = rcp[0:npart, 0:1]
                        r1_ = rcp[0:npart, 1:2]
                        P.op("dve", "reciprocal", r0, O0[:, 128:129], reads=rk, writes=[("rcp", 0)])
                        P.op("dve", "reciprocal", r1_, O1[:, 128:129], reads=rk, writes=[("rcp", 1)])
                        P.op("dve", "tensor_tensor", r1_, r1_, lam_ap[0:npart, :], ALU.mult,
                             reads=[("rcp", 1), ("lam",)], writes=[("rcp", 1)])
                        P.op("dve", "tensor_scalar", t1[0:npart, :], O1[:, 0:128], r1_, None, ALU.mult,
                             reads=rk + [("rcp", 1)], writes=[("t1",)])
                        P.op("dve", "scalar_tensor_tensor", odt[si_][0:npart, :], O0[:, 0:128], r0,
                             t1[0:npart, :], ALU.mult, ALU.subtract,
                             reads=rk + [("rcp", 0), ("t1",)], writes=[("od", id(odt), si_)])
                        P.op("dve", "scalar_tensor_tensor", sq[0:npart, :], odt[si_][0:npart, :], 1.0,
                             odt[si_][0:npart, :], ALU.mult, ALU.mult,
                             accum_out=den[0:npart, d0 + si_:d0 + si_ + 1],
                             reads=[("od", id(odt), si_)], writes=[("sq",), dk + (si_,)])

            def s23():
                if fox:
                    return
                P.op("act", "activation", den[0:npart, d0 + 4:d0 + 4 + n], den[0:npart, d0:d0 + n], AF.Ln,
                     scale=1.0 / 128, bias=epsc[0:npart, 0:1],
                     reads=[dk + (i_,) for i_ in range(n)] + [("epsc",)], writes=[dk + ("r",)])
                P.op("act", "activation", den[0:npart, d0 + 4:d0 + 4 + n], den[0:npart, d0 + 4:d0 + 4 + n], AF.Exp,
                     scale=-0.5, reads=[dk + ("r",)], writes=[dk + ("r",)])
                for si_, (Oreg, out_ap, slot) in enumerate(items):
                    P.op("dve", "scalar_tensor_tensor", out_ap, odt[si_][0:npart, :],
                         den[0:npart, d0 + 4 + si_:d0 + 5 + si_], g8[0:npart, :], ALU.mult, ALU.mult,
                         reads=[("od", id(odt), si_), dk + ("r",), ("g8",)], writes=[slot + (0,), slot + (1,)])

            return s1, s23

        def load_weights(u):
            sl = u % 2
            P.dma("pool", wqkb[sl][:, :, :], wqk[u].rearrange("p (k j) -> p k j", k=8), writes=[("wqkb", sl)])
            P.dma("pool", wkvb[sl][:, :, :], wkv[u].rearrange("p (k j) -> p k j", k=8), writes=[("wkvb", sl)])
            P.dma("pool", wob[u % 4][:, :], wo[u], writes=[("wob", u % 4)])

        def load_sample(sj):
            u, b = divmod(sj, 4)
            fox = u < 4
            cs_ = sj % 2
            kc_, vc_ = kTc[cs_], Vc[cs_]
            kkc, kvc = ("kTc", cs_), ("Vc", cs_)
            for m in range(2):
                P.dma("pool", kc_[0:64, m, :], kTc_d[u, b, 64 * m:64 * m + 64, :], writes=[kkc])
            if fox:
                if sj < 2:
                    P.op("pool", "memset", kc_[64:70, :, :], 1.0, writes=[kkc])
                    P.op("pool", "memset", vc_[:, :, 64:65], 1.0, writes=[kvc])
                    P.op("pool", "memset", vc_[:, :, 130:131], 1.0, writes=[kvc])
                for m in range(2):
                    P.dma("sp", kc_[67:70, m, :], caugS[1, b, 2 * u + m, :, 0:TP], reads=[("caug", 0), ("caug", 1), ("caug", 2), ("caug", 3)], writes=[kkc])
                P.dma("pool", vc_[:, :, :].rearrange("p j (m c) -> p j m c", m=2)[:, :, :, 0:64],
                      vc_d[u, b].rearrange("p j (m d) -> p j m d", m=2), writes=[kvc])
            else:
                for m in range(2):
                    P.dma("pool", kc_[64:70, m, :], alk_d[u - 4, :, 0:TP], writes=[kkc])
                P.dma("pool", vc_[:, :, 0:128], vc_d[u, b], writes=[kvc])
                if sj < 18:
                    P.op("pool", "memset", vc_[:, :, 128:129], 1.0, writes=[kvc])

        if DBG.get("nsb", 4) == 4:
            load_sample(0)
            load_sample(1)

        for u in DBG.get("units", range(8)):
            fox = u < 4
            W = 65 if fox else 129
            sl = u % 2
            q_, k_, V_, Vn_ = qT[0], kT[0], Vt[0], Vn[0]
            wq_, wk_ = wqkb[sl], wkvb[sl]
            kq, kk, kV, kVn = ("qT", 0), ("kT", 0), ("Vt", 0), ("Vn", 0)
            kqa = [("qTa", i_) for i_ in range(4)]
            kka = [("kTa", i_) for i_ in range(4)]
            vall = [("Vt", t_, m_) for t_ in range(16) for m_ in range(2)]

            def vcols(m):
                return (66 * m, 65) if fox else (0, 129)

            if u == 0:
                load_weights(0)
            if "aug" in DBG.get("skip", ()):
                pass
            elif fox:
                if u == 0:
                    P.op("pool", "memset", q_[64:70, :, :], 1.0, writes=kqa)
                    P.op("pool", "memset", k_[64:70, :, :], 1.0, writes=kka)
                for m in range(2):
                    h = 2 * u + m
                    P.dma("sp", q_[64:67, m, 0:TP], caugP[0, h], reads=[("caug", 0), ("caug", 1), ("caug", 2), ("caug", 3)], writes=[kqa[m]])
                    P.dma("sp", k_[67:70, m, 0:TP], caugP[1, h], reads=[("caug", 0), ("caug", 1), ("caug", 2), ("caug", 3)], writes=[kka[m]])
                    P.dma("sp", q_[64:67, m, TP:T].rearrange("r (b s) -> r b s", b=4),
                          caugS[0, :, h, :, TP:TP + 32].rearrange("b r s -> r b s"), reads=[("caug", 0), ("caug", 1), ("caug", 2), ("caug", 3)], writes=[kqa[2 + m]])
                    P.dma("sp", k_[67:70, m, TP:T].rearrange("r (b s) -> r b s", b=4),
                          caugS[1, :, h, :, TP:TP + 32].rearrange("b r s -> r b s"), reads=[("caug", 0), ("caug", 1), ("caug", 2), ("caug", 3)], writes=[kka[2 + m]])
                if u == 0:
                    P.op("pool", "memset", V_[:, :, 64:65], 1.0, writes=vall)
                    P.op("pool", "memset", V_[:, :, 130:131], 1.0, writes=vall)
                    P.op("pool", "memset", Vn_[:, :, 64:65], 1.0, writes=[kVn])
                    P.op("pool", "memset", Vn_[:, :, 130:131], 1.0, writes=[kVn])
            else:
                h = u - 4
                for m in range(2):
                    P.dma("pool", q_[64:70, m, :], alq_d[h], writes=[kqa[m], kqa[2 + m]])
                    P.dma("pool", k_[64:70, m, :], alk_d[h], writes=[kka[m], kka[2 + m]])
                if u == 4:
                    P.op("pool", "memset", V_[:, :, 128:129], 1.0, writes=vall)
                    P.op("pool", "memset", Vn_[:, :, 128:129], 1.0, writes=[kVn])

            def fm_block(ci, dst, dkey, scl, s0, sn):
                bk = sbank[0] % 2
                sbank[0] += 1
                ps = PS[bk]
                hk = [kk_ for t in tiles_of(s0, sn) for kk_ in hTk(t)]
                for k in range(8):
                    P.op("pe", "matmul", ps[:, 0:sn], wq_[:, k, ci * 128:(ci + 1) * 128], hT[:, k, s0:s0 + sn],
                         start=(k == 0), stop=(k == 7), reads=[("wqkb", sl)] + hk, writes=[("ps", bk)])
                P.op("act", "activation", dst[0:64, 0, s0:s0 + sn], ps[0:64, 0:sn], AF.Copy, scale=scl,
                     reads=[("ps", bk)], writes=[dkey])
                P.op("dve", "tensor_scalar", dst[0:64, 1, s0:s0 + sn], ps[64:128, 0:sn], scl, None, ALU.mult,
                     reads=[("ps", bk)], writes=[dkey])

            def tm_tile(t):
                bk = 2 + t % 2
                ps = PS[bk]
                for k in range(8):
                    P.op("pe", "matmul", ps[:, 0:256], hT[:, k, t * 128:(t + 1) * 128], wk_[:, k, :],
                         start=(k == 0), stop=(k == 7), reads=hTk(t) + [("wkvb", sl)], writes=[("ps", bk)])
                si = kvcnt[0] % 3
                kvcnt[0] += 1
                P.op("act", "activation", kvst[si][:, :], ps[:, 0:256], AF.Copy,
                     reads=[("ps", bk)], writes=[("kvst", si)])
                P.dma("sp", kv_d[u, t * 128:(t + 1) * 128, :], kvst[si][:, :], reads=[("kvst", si)])
                if t < 16:
                    if fox:
                        for m in range(2):
                            P.op("act", "activation", V_[:, t, 66 * m:66 * m + 64], ps[:, 128 + 64 * m:192 + 64 * m],
                                 AF.Copy, reads=[("ps", bk)], writes=[(kV[0], t, m)])
                    else:
                        P.op("act", "activation", V_[:, t, 0:128], ps[:, 128:256], AF.Copy,
                             reads=[("ps", bk)], writes=[(kV[0], t, 0), (kV[0], t, 1)])

            fm_list = [(ci, dst, dkey, scl, s0, sn) for ci, (dst, dkey, scl) in enumerate(((q_, kq, 0.125), (k_, kk, 1.0)))
                       for (s0, sn) in SUBS]
            tms = list(range(NT))
            def pop_out(n):
                for _ in range(n):
                    if pending_out:
                        pending_out.pop(0)()

            for fi, fa_ in enumerate(fm_list):
                fm_block(*fa_)
                pop_out(2)
                for _ in range(2 if fi < 7 else 1):
                    if tms:
                        tm_tile(tms.pop(0))
                        pop_out(1)
            while tms:
                tm_tile(tms.pop(0))
                pop_out(1)
            pop_out(len(pending_out))
            for b in range(4 if "stm" not in DBG.get("skip", ()) else 0):
                bk = 2 + b % 2
                ps = PS[bk]
                c0 = TP + 32 * b
                for k in range(8):
                    P.op("pe", "matmul", ps[0:32, 0:256], hT[:, k, c0:c0 + 32], wk_[:, k, :],
                         start=(k == 0), stop=(k == 7), reads=hTk(16) + [("wkvb", sl)], writes=[("ps", bk)])
                if fox:
                    for m in range(2):
                        P.op("act", "activation", Vn_[:, b, 66 * m:66 * m + 64], ps[0:32, 128 + 64 * m:192 + 64 * m],
                             AF.Copy, reads=[("ps", bk)], writes=[kVn])
                else:
                    P.op("act", "activation", Vn_[:, b, 0:128], ps[0:32, 128:256], AF.Copy,
                         reads=[("ps", bk)], writes=[kVn])

            if u + 1 < 8:
                load_weights(u + 1)

            OSETS = [((PS[4], ("ps", 4)), (PS[5], ("ps", 5)), (PS[6], ("ps", 6))),
                     ((PS[2], ("ps", 2)), (PS[3], ("ps", 3)), (TRF, ("TR", 0)))]
            SBK = [0, 1, 6] if fox else [0, 1]

            def prompt_qt(qt, deferred):
                oset = OSETS[qt % 2]
                rk = [oset[0][1], oset[1][1]] + ([] if fox else [oset[2][1]])

                def Oreg_p(m, sub):
                    if fox:
                        return oset[m][0][:, sub * 66: sub * 66 + W]
                    if sub < 3:
                        return oset[m][0][:, sub * 130: sub * 130 + W]
                    return oset[2][0][:, m * 130: m * 130 + W]

                def okey(m, sub):
                    return oset[m][1] if (sub < 3 or fox) else oset[2][1]

                q0 = 512 * qt
                steps = [(m, j) for m in range(2) for j in range(4 * qt + 4)]
                pend = []
                started = set()

                def do_S(m, j):
                    jj = j - 4 * qt
                    qs = q0 + (128 * jj if jj > 0 else 0)
                    n = q0 + 512 - qs
                    bk = SBK[sbank[0] % len(SBK)]
                    sbank[0] += 1
                    ps = PS[bk]
                    diag = jj >= 0
                    P.op("pe", "matmul", ps[:, 0:n], k_[:, m, 128 * j:128 * j + 128], q_[:, m, qs:qs + n],
                         start=True, stop=not diag, reads=[kk, kq] + kqa + kka, writes=[("ps", bk)])
                    if diag:
                        di = 0 if fox else 1 + (u - 4)
                        P.op("pe", "matmul", ps[:, 0:128], identb[:, :], dm[:, di, :], start=False, stop=True,
                             reads=[("identb",), ("dm",)], writes=[("ps", bk)])
                    pi = ptc[0] % 4
                    ptc[0] += 1
                    P.op("act", "activation", pT[pi][:, 0:n], ps[:, 0:n], AF.Exp, reads=[("ps", bk)],
                         writes=[("pT", pi)])
                    return (m, j, pi, max(jj, 0))

                def do_PV(m, j, pi, jj0):
                    c0, cw = vcols(m)
                    for sub in range(jj0, 4):
                        ok = okey(m, sub)
                        first = ok not in started
                        started.add(ok)
                        last = (j == 4 * qt + sub)
                        P.op("pe", "matmul", Oreg_p(m, sub), pT[pi][:, (sub - jj0) * 128:(sub - jj0) * 128 + 128],
                             V_[:, j, c0:c0 + cw], start=first, stop=last, skip_group_check=True,
                             reads=[("pT", pi), ("Vt", j, 0), ("Vt", j, 1)], writes=[ok])

                for i_, st_ in enumerate(steps):
                    pend.append(do_S(*st_))
                    if len(pend) > len(SBK) - 1:
                        do_PV(*pend.pop(0))
                        for _d in range(NDUMMY):
                            P.op("pe", "matmul", PS[6][:, 260:512], identb[:, :],
                                 dm[:, 0:2, :].rearrange("p a b -> p (a b)")[:, 0:252],
                                 start=False, stop=False, skip_group_check=True, reads=[("identb",), ("dm",)])
                    if i_ == 3:
                        for f_ in deferred[0]:
                            f_()
                    if i_ == min(11, len(steps) - 1):
                        for f_ in deferred[1]:
                            f_()
                while pend:
                    do_PV(*pend.pop(0))

                items = [((lambda m, sub=sub: Oreg_p(m, sub)), ou[:, 4 * qt + sub, :], ("ou", 4 * qt + sub))
                         for sub in range(4)]
                return finalize(u, 128, items, rk, od)

            def sample_attn(b, osb):
                cs_ = (u * 4 + b) % 2
                kc_, vc_ = kTc[cs_], Vc[cs_]
                kkc, kvc = ("kTc", cs_), ("Vc", cs_)
                qc0 = TP + 32 * b
                OS = PS[osb]
                first = [True]

                def Oreg_s(m):
                    return OS[0:32, m * 130:m * 130 + W]

                def S_grp(g):
                    bk = sbank[0] % 2
                    sbank[0] += 1
                    ps = PS[bk]
                    for jj in range(4):
                        j = 4 * g + jj
                        for m in range(2):
                            cidx = (jj * 2 + m) * 32
                            P.op("pe", "matmul", ps[:, cidx:cidx + 32], kc_[:, m, 128 * j:128 * j + 128],
                                 q_[:, m, qc0:qc0 + 32], start=True, stop=True, skip_group_check=True,
                                 reads=[kkc, kq] + kqa, writes=[("ps", bk)])
                    pi = ptc[0] % 4
                    ptc[0] += 1
                    P.op("act", "activation", pT[pi][:, 0:256], ps[:, 0:256], AF.Exp, reads=[("ps", bk)],
                         writes=[("pT", pi)])
                    return pi

                def PV_grp(g, pi):
                    for jj in range(4):
                        j = 4 * g + jj
                        for m in range(2):
                            cidx = (jj * 2 + m) * 32
                            c0, cw = vcols(m)
                            P.op("pe", "matmul", Oreg_s(m), pT[pi][:, cidx:cidx + 32], vc_[:, j, c0:c0 + cw],
                                 start=first[0], stop=False, skip_group_check=True,
                                 reads=[("pT", pi), kvc], writes=[("ps", osb)])
                            first[0] = False

                def S_new():
                    bk = sbank[0] % 2
                    sbank[0] += 1
                    ps = PS[bk]
                    di = 0 if fox else 1 + (u - 4)
                    for m in range(2):
                        P.op("pe", "matmul", ps[0:32, m * 32:m * 32 + 32], k_[:, m, qc0:qc0 + 32],
                             q_[:, m, qc0:qc0 + 32], start=True, stop=False, skip_group_check=True,
                             reads=[kk, kq] + kqa + kka, writes=[("ps", bk)])
                        P.op("pe", "matmul", ps[0:32, m * 32:m * 32 + 32], identb[0:32, 0:32], dm[0:32, di, 0:32],
                             start=False, stop=True, skip_group_check=True,
                             reads=[("identb",), ("dm",)], writes=[("ps", bk)])
                    pi = ptc[0] % 4
                    ptc[0] += 1
                    P.op("act", "activation", pT[pi][0:32, 0:64], ps[0:32, 0:64], AF.Exp, reads=[("ps", bk)],
                         writes=[("pT", pi)])
                    return pi

                def PV_new(pi):
                    for m in range(2):
                        c0, cw = vcols(m)
                        P.op("pe", "matmul", Oreg_s(m), pT[pi][0:32, m * 32:m * 32 + 32], Vn_[0:32, b, c0:c0 + cw],
                             start=False, stop=True, skip_group_check=True,
                             reads=[("pT", pi), kVn], writes=[("ps", osb)])

                pis = [S_grp(0)]
                for g in range(1, 4):
                    pis.append(S_grp(g))
                    PV_grp(g - 1, pis[g - 1])
                pn = S_new()
                PV_grp(3, pis[3])
                PV_new(pn)
                return finalize(u, 32, [(Oreg_s, oS[0:32, b, :], ("oS", b))], [("ps", osb)], ods)


            dfr = ([], [])
            for qt in range(4):
                nxt = ([], [])
                if qt < DBG.get("nqt", 4):
                    p1, p23 = prompt_qt(qt, dfr)
                    nxt[0].append(p1)
                    nxt[1].append(p23)
                else:
                    for f_ in dfr[0] + dfr[1]:
                        f_()
                if qt < DBG.get("nsb", 4):
                    s1_, s23_ = sample_attn(qt, 2 if qt % 2 == 0 else 4)
                    s1_()
                    nxt[1].append(s23_)
                    if DBG.get("nsb", 4) == 4 and u * 4 + qt + 2 < 32:
                        load_sample(u * 4 + qt + 2)
                dfr = nxt
            for f_ in dfr[0] + dfr[1]:
                f_()
            if DBG.get("no_out"):
                continue
            uu = u % 2
            for g in range(5):
                trv, trk = TRU[g % 2]
                if g < 4:
                    for jj in range(4):
                        t = g * 4 + jj
                        P.op("pe", "transpose", trv[:, jj * 128:(jj + 1) * 128], ou[:, t, :], identb[:, :],
                             reads=[("ou", t, 0), ("ou", t, 1), ("identb",)], writes=[trk])
                    P.op("act", "activation", oT[:, uu, g * 512:(g + 1) * 512], trv[:, 0:512], AF.Copy,
                         reads=[trk], writes=[("oT", uu, g)])
                else:
                    for b in range(4):
                        P.op("pe", "transpose", trv[:, b * 32:(b + 1) * 32], oS[0:32, b, :], identb[0:32, 0:32],
                             reads=[("oS", b, 0), ("oS", b, 1), ("identb",)], writes=[trk])
                    P.op("act", "activation", oT[:, uu, TP:T], trv[:, 0:128], AF.Copy,
                         reads=[trk], writes=[("oT", uu, 4)])
            if uu == 1:
                def mk_out(t, dh, ub):
                    def f():
                        bk = 4 + pocnt[0] % 2
                        pocnt[0] += 1
                        po = PS[bk]
                        for w in range(2):
                            wsl = (ub - 1 + w) % 4
                            P.op("pe", "matmul", po[:, :], oT[:, w, t * 128:(t + 1) * 128],
                                 wob[wsl][:, dh * 512:(dh + 1) * 512], start=(w == 0), stop=(w == 1),
                                 reads=[("oT", w, t // 4), ("wob", wsl)], writes=[("ps", bk)])
                        P.op("dve", "tensor_tensor", x[:, t, dh * 512:(dh + 1) * 512], po[:, :],
                             x[:, t, dh * 512:(dh + 1) * 512], ALU.add,
                             reads=[("ps", bk), ("x", t)], writes=[("x", t)])
                    return f
                for t in range(NT):
                    for dh in range(2):
                        pending_out.append(mk_out(t, dh, u))
                if u == 7:
                    while pending_out:
                        pending_out.pop(0)()

        stk[0].close()
        P.realloc(U_NAMES, old_names, fscr[0:1, 1:2])


    if stage >= 2:
        phase_c()
    if stage >= 3:
        phase_u()
    elif stage >= 2:
        stk[0].close()
        P.realloc(C_NAMES, old_names, fscr[0:1, 1:2])

    if stage >= 4:
        ffn(w2a, w2b, 2, "b")

    gfin = nc.alloc_sbuf_tensor("s_gfin", [128, DM], F32)
    P.realloc(old_names, ["gfin"], fscr[0:1, 3:4])
    P.dma("sp", gfin[:, :], gfin_d[0:1, :].broadcast_to([128, DM]), writes=[("gfin",)])
    y_v = y_d.rearrange("(j p) d -> p j d", p=128)
    rstd_all()
    for t in range(NT):
        P.op("dve", "scalar_tensor_tensor", x[:, t, :], x[:, t, :], rstd[:, t:t + 1], gfin[:, :], ALU.mult, ALU.mult,
             reads=[("x", t), ("rstd",), ("gfin",)], writes=[("x", t)])
        P.dma("sp", y_v[:, t, :], x[:, t, :], reads=[("x", t)])

    P.finalize_and_emit()
    return nc


def _cols2tile(w):
    n = w.shape[1]
    return np.ascontiguousarray(w.reshape(8, 128, n).transpose(1, 0, 2))


def _ffn_a(w):
    G = w[:, :DFF].reshape(8, 128, NCH, 128).transpose(2, 1, 0, 3)
    U = w[:, DFF:].reshape(8, 128, NCH, 128).transpose(2, 1, 0, 3)
    return np.ascontiguousarray(np.concatenate([G, U], axis=-1)).reshape(NCH, 128, 2048)


def _ffn_b(w):
    return np.ascontiguousarray(w.reshape(NCH, 128, 1024).transpose(1, 0, 2))


def _const_tables():
    pos = np.concatenate([np.arange(TP), TP + (np.arange(T - TP) % 32)]).astype(np.float32)
    alq = np.zeros((4, 6, T), np.float32)
    alk = np.zeros((4, 6, T), np.float32)
    for h, m in enumerate(SLOPES):
        alq[h, 0] = -m * 64.0 * np.floor(pos / 64.0)
        alq[h, 1] = -m * np.mod(pos, 64.0)
        alq[h, 2] = 1.0
        alq[h, 3] = 1.0
        alk[h, 0] = 1.0
        alk[h, 1] = 1.0
        alk[h, 2] = m * 64.0 * np.floor(pos / 64.0)
        alk[h, 3] = m * np.mod(pos, 64.0)
    s = np.arange(128)[:, None]
    t = np.arange(128)[None, :]
    dmask = np.zeros((128, 5, 128), np.float32)
    dmask[:, 0, :] = np.where(s > t, NEG, 0.0)
    for h, m in enumerate(SLOPES):
        dmask[:, 1 + h, :] = np.where(s > t, -2.0 * m * (s - t), 0.0) + np.where(s // 64 > t // 64, NEG, 0.0)
    return alq, alk, dmask


_NC_CACHE = {}


def kernel(x_prompt, x_sample, cache_fox_k, cache_fox_v, cache_fox_logf, cache_diff_k, cache_diff_v,
           norm_ffn1, w_ffn1_in, w_ffn1_out, norm_mix, w_in, b_forget,
           lambda_q1, lambda_k1, lambda_q2, lambda_k2, diff_subln, w_out,
           norm_ffn2, w_ffn2_in, w_ffn2_out, norm_final):
    f = lambda a: np.asarray(a, dtype=np.float32)
    x_prompt, x_sample = f(x_prompt), f(x_sample)
    cache_fox_k, cache_fox_v, cache_fox_logf = f(cache_fox_k), f(cache_fox_v), f(cache_fox_logf)
    cache_diff_k, cache_diff_v = f(cache_diff_k), f(cache_diff_v)
    w_in0 = f(w_in)[0]
    w_out0 = f(w_out)[0]
    shared = {}
    shared["w1a"] = _ffn_a(f(w_ffn1_in)[0])
    shared["w1b"] = _ffn_b(f(w_ffn1_out)[0])
    shared["w2a"] = _ffn_a(f(w_ffn2_in)[0])
    shared["w2b"] = _ffn_b(f(w_ffn2_out)[0])
    qa, ka, va = w_in0[:, 0:512], w_in0[:, 512:1024], w_in0[:, 1024:1536]
    fa = w_in0[:, 1536:1544]
    qb, kb, vb = w_in0[:, 1544:2056], w_in0[:, 2056:2568], w_in0[:, 2568:3080]
    wqk = np.zeros((8, 128, 8, 256), np.float32)
    wkv = np.zeros((8, 128, 8, 256), np.float32)
    for u in range(8):
        q_, k_, v_ = (qa, ka, va) if u < 4 else (qb, kb, vb)
        c = 128 * (u % 4)
        wqk[u, :, :, 0:128] = _cols2tile(q_[:, c:c + 128])
        wqk[u, :, :, 128:256] = _cols2tile(k_[:, c:c + 128])
        wkv[u, :, :, 0:128] = _cols2tile(k_[:, c:c + 128])
        wkv[u, :, :, 128:256] = _cols2tile(v_[:, c:c + 128])
    shared["wqk"] = wqk.reshape(8, 128, 2048)
    shared["wkv"] = wkv.reshape(8, 128, 2048)
    shared["wfa"] = _cols2tile(fa).reshape(128, 64)
    shared["wo"] = np.ascontiguousarray(w_out0.reshape(8, 128, 1024))
    gc = np.stack([f(norm_ffn1)[0], f(norm_mix)[0], f(norm_ffn2)[0]], 0)
    shared["gvec"] = np.ascontiguousarray(gc)
    shared["gcols"] = np.ascontiguousarray(gc.reshape(3, 8, 128).transpose(2, 0, 1)).reshape(128, 24)
    shared["gfin"] = f(norm_final).reshape(1, DM)
    shared["bfg"] = np.ascontiguousarray(np.tile(f(b_forget)[0], NT)).reshape(1, NT * 8)
    shared["lamv"] = np.concatenate([f(lambda_q1)[0], f(lambda_k1)[0], f(lambda_q2)[0], f(lambda_k2)[0]]).reshape(1, 256)
    shared["subln"] = f(diff_subln).reshape(1, 128)
    shared["ident"] = np.eye(128, dtype=np.float32)
    alq, alk, dmask = _const_tables()
    shared["alq"], shared["alk"], shared["dmask"] = alq, alk, dmask

    in_maps = []
    for i in range(8):
        b0 = 4 * i
        m = dict(shared)
        m["xin"] = np.ascontiguousarray(np.concatenate([x_prompt[i], x_sample[b0:b0 + 4].reshape(128, DM)], 0))
        fk = cache_fox_k[0, b0:b0 + 4].reshape(4, TP, 4, 128).transpose(2, 0, 3, 1)
        dk = cache_diff_k[0, b0:b0 + 4].reshape(4, TP, 4, 128).transpose(2, 0, 3, 1)
        m["kTc"] = np.ascontiguousarray(np.concatenate([fk, dk], 0))
        fv = cache_fox_v[0, b0:b0 + 4].reshape(4, 16, 128, 4, 128).transpose(3, 0, 2, 1, 4)
        dv = cache_diff_v[0, b0:b0 + 4].reshape(4, 16, 128, 4, 128).transpose(3, 0, 2, 1, 4)
        m["vc"] = np.ascontiguousarray(np.concatenate([fv, dv], 0))
        m["plf"] = np.ascontiguousarray(cache_fox_logf[0, b0:b0 + 4].transpose(0, 2, 1)).reshape(32, TP)
        in_maps.append(m)

    if "nc" not in _NC_CACHE:
        _NC_CACHE["nc"] = build_nc()
    nc = _NC_CACHE["nc"]
    res = run_bass_kernel_spmd(nc, in_maps, core_ids=list(range(8)))
    R = res.results

    y_prompt = np.zeros((8, TP, DM), np.float32)
    y_sample = np.zeros((32, 32, DM), np.float32)
    nfk_p = np.zeros((1, 8, TP, 8, 64), np.float32)
    nfv_p = np.zeros((1, 8, TP, 8, 64), np.float32)
    nlf_p = np.zeros((1, 8, TP, 8), np.float32)
    ndk_p = np.zeros((1, 8, TP, 4, 2, 64), np.float32)
    ndv_p = np.zeros((1, 8, TP, 4, 128), np.float32)
    nfk_s = np.zeros((1, 32, 32, 8, 64), np.float32)
    nfv_s = np.zeros((1, 32, 32, 8, 64), np.float32)
    nlf_s = np.zeros((1, 32, 32, 8), np.float32)
    ndk_s = np.zeros((1, 32, 32, 4, 2, 64), np.float32)
    ndv_s = np.zeros((1, 32, 32, 4, 128), np.float32)
    for i in range(8):
        b0 = 4 * i
        y = R[i]["y"]
        kv = R[i]["kv"]
        lfo = R[i]["lfo"]
        y_prompt[i] = y[:TP]
        y_sample[b0:b0 + 4] = y[TP:].reshape(4, 32, DM)
        nlf_p[0, i] = lfo[:TP]
        nlf_s[0, b0:b0 + 4] = lfo[TP:].reshape(4, 32, 8)
        for u in range(4):
            nfk_p[0, i, :, 2 * u:2 * u + 2, :] = kv[u, :TP, 0:128].reshape(TP, 2, 64)
            nfv_p[0, i, :, 2 * u:2 * u + 2, :] = kv[u, :TP, 128:256].reshape(TP, 2, 64)
            nfk_s[0, b0:b0 + 4, :, 2 * u:2 * u + 2, :] = kv[u, TP:, 0:128].reshape(4, 32, 2, 64)
            nfv_s[0, b0:b0 + 4, :, 2 * u:2 * u + 2, :] = kv[u, TP:, 128:256].reshape(4, 32, 2, 64)
            ndk_p[0, i, :, u] = kv[4 + u, :TP, 0:128].reshape(TP, 2, 64)
            ndv_p[0, i, :, u] = kv[4 + u, :TP, 128:256]
            ndk_s[0, b0:b0 + 4, :, u] = kv[4 + u, TP:, 0:128].reshape(4, 32, 2, 64)
            ndv_s[0, b0:b0 + 4, :, u] = kv[4 + u, TP:, 128:256].reshape(4, 32, 128)
    return (y_prompt, y_sample, nfk_p, nfv_p, nlf_p, ndk_p, ndv_p, nfk_s, nfv_s, nlf_s, ndk_s, ndv_s)
```
